# Optimizing a Trainium2 kernel written in Bass

```python
import jax, jax.numpy as jnp
from jax import lax
import numpy as np

D_MODEL = 1024
BATCH = 8
SEQ = 8192
DEPTH = 1

CHUNK = 64
D_SHORT = D_MODEL
D_CONF = D_MODEL
SHORT_K = 3
CONF_K = 31
D_FF = -(-8 * D_MODEL // (3 * 256)) * 256
N_MOD = 6
EPS = 1e-6
LN_EPS = 1e-5
SPLITS = (D_SHORT, 2 * D_SHORT, 3 * D_SHORT,
          3 * D_SHORT + D_CONF, 3 * D_SHORT + 2 * D_CONF,
          3 * D_SHORT + 2 * D_CONF + D_MODEL)
D_IN = 3 * D_SHORT + 2 * D_CONF + 2 * D_MODEL

kernel_name = 'hybrid_shortconv_conformer_gated_block'


def rms_norm(x, g):
    xf = x.astype(jnp.float32)
    y = xf * lax.rsqrt(jnp.mean(xf * xf, axis=-1, keepdims=True) + EPS)
    return (y * g.astype(jnp.float32)).astype(x.dtype)


def layer_norm(x, g, b):
    xf = x.astype(jnp.float32)
    mu = jnp.mean(xf, axis=-1, keepdims=True)
    var = jnp.mean(jnp.square(xf - mu), axis=-1, keepdims=True)
    y = (xf - mu) * lax.rsqrt(var + LN_EPS)
    return (y * g.astype(jnp.float32) + b.astype(jnp.float32)).astype(x.dtype)


def causal_dwconv(u, w):
    k, ch = w.shape
    return lax.conv_general_dilated(
        u, w.astype(u.dtype)[:, None, :], window_strides=(1,), padding=[(k - 1, 0)],
        dimension_numbers=('NWC', 'WIO', 'NWC'), feature_group_count=ch)


def setup_inputs(seed: int = 0) -> dict:
    key = jax.random.key(seed)
    ks = jax.random.split(key, 20)
    n = jax.random.normal
    d = D_MODEL
    return {
        'x': n(ks[0], (BATCH, SEQ, d), jnp.float32),
        'c': n(ks[1], (BATCH, d), jnp.float32),
        'w_ada': n(ks[2], (DEPTH, d, N_MOD * d), jnp.float32) * (0.5 * d ** -0.5),
        'b_ada': n(ks[3], (DEPTH, N_MOD * d), jnp.float32) * 0.02,
        'norm_mix_g': 1.0 + 0.05 * n(ks[4], (DEPTH, d), jnp.float32),
        'w_in': n(ks[5], (DEPTH, d, D_IN), jnp.float32) * d ** -0.5,
        'conv_short_w': n(ks[6], (DEPTH, SHORT_K, D_SHORT), jnp.float32) * SHORT_K ** -0.5,
        'w_short_out': n(ks[7], (DEPTH, D_SHORT, d), jnp.float32) * D_SHORT ** -0.5,
        'conv_conf_w': n(ks[8], (DEPTH, CONF_K, D_CONF), jnp.float32) * CONF_K ** -0.5,
        'conv_conf_b': n(ks[9], (DEPTH, D_CONF), jnp.float32) * 0.02,
        'conf_ln_g': 1.0 + 0.05 * n(ks[10], (DEPTH, D_CONF), jnp.float32),
        'conf_ln_b': n(ks[11], (DEPTH, D_CONF), jnp.float32) * 0.02,
        'w_conf_out': n(ks[12], (DEPTH, D_CONF, d), jnp.float32) * D_CONF ** -0.5,
        'w_o': n(ks[13], (DEPTH, d, d), jnp.float32) * d ** -0.5,
        'norm_ffn_g': 1.0 + 0.05 * n(ks[14], (DEPTH, d), jnp.float32),
        'w_ffn_in': n(ks[15], (DEPTH, d, 2 * D_FF), jnp.float32) * d ** -0.5,
        'w_ffn_out': n(ks[16], (DEPTH, D_FF, d), jnp.float32) * D_FF ** -0.5,
        'final_norm_g': 1.0 + 0.05 * n(ks[17], (d,), jnp.float32),
    }


def reference(x, c, w_ada, b_ada, norm_mix_g, w_in, conv_short_w, w_short_out,
              conv_conf_w, conv_conf_b, conf_ln_g, conf_ln_b, w_conf_out, w_o,
              norm_ffn_g, w_ffn_in, w_ffn_out, final_norm_g):
    for l in range(DEPTH):
        mod = jax.nn.silu(c) @ w_ada[l] + b_ada[l]
        sh1, sc1, g1, sh2, sc2, g2 = jnp.split(mod[:, None, :], N_MOD, axis=-1)

        h = rms_norm(x, norm_mix_g[l]) * (1.0 + sc1) + sh1
        proj = jnp.einsum('bsd,de->bse', h, w_in[l])
        b_s, c_s, v_s, v_c, gl_c, gate_br = jnp.split(proj, SPLITS[:-1], axis=-1)

        y_a = b_s * causal_dwconv(c_s * v_s, conv_short_w[l])
        y_a = jnp.einsum('bsc,cd->bsd', y_a, w_short_out[l])

        u = v_c * jax.nn.sigmoid(gl_c)
        u = causal_dwconv(u, conv_conf_w[l]) + conv_conf_b[l]
        u = jax.nn.silu(layer_norm(u, conf_ln_g[l], conf_ln_b[l]))
        y_b = jnp.einsum('bsc,cd->bsd', u, w_conf_out[l])

        g_a, g_b = jnp.split(jax.nn.sigmoid(gate_br), 2, axis=-1)
        mix = jnp.einsum('bsd,de->bse', g_a * y_a + g_b * y_b, w_o[l])
        x = x + g1 * mix

        h2 = rms_norm(x, norm_ffn_g[l]) * (1.0 + sc2) + sh2
        a, bgate = jnp.split(jnp.einsum('bsd,df->bsf', h2, w_ffn_in[l]), 2, axis=-1)
        x = x + g2 * jnp.einsum('bsf,fd->bsd', jax.nn.silu(a) * bgate, w_ffn_out[l])

    return rms_norm(x, final_norm_g)
```

```python
import numpy as np
import ml_dtypes
import concourse.bass as bass
import concourse.mybir as mybir
from concourse.bass_utils import run_bass_kernel_spmd

F32 = mybir.dt.float32
BF16 = mybir.dt.bfloat16
AF = mybir.ActivationFunctionType
ALU = mybir.AluOpType

D = 1024
DFF = 2816
DIN = 7168
NMOD = 6
SEQ = 8192
T = 512
NT = T // 128
KC = D // 128
FC = DFF // 128
EPS = 1e-6
LN_EPS = 1e-5
ATOM = 256
NSLOT = 7
NPE = 16
SLOT_BYTES = 8192
NLANE = 8


class _Op:
    __slots__ = ("eng", "fn", "deps", "idx", "is_dma", "signals", "dma_n", "count")


class Sched:
    ENGS = ("pe", "act", "dve", "pool", "sp")

    def __init__(self):
        self.ops = []
        self.by_eng = {e: [] for e in self.ENGS}
        self.atoms = {}
        self.dma_ops = {e: [] for e in self.ENGS}

    def op(self, eng, fn, reads=(), writes=(), dma=False):
        o = _Op()
        o.eng, o.fn, o.is_dma, o.signals, o.count, o.dma_n = eng, fn, dma, False, 0, -1
        o.idx = len(self.ops)
        deps = set()
        for k in reads:
            a = self.atoms.get(k)
            if a is not None and a[0] >= 0:
                deps.add(a[0])
        for k in writes:
            a = self.atoms.get(k)
            if a is not None:
                if a[0] >= 0:
                    deps.add(a[0])
                deps.update(a[1].values())
                deps.update(a[2])
        for k in reads:
            a = self.atoms.get(k)
            if a is None:
                a = self.atoms[k] = [-1, {}, []]
            if dma:
                a[2].append(o.idx)
            else:
                a[1][eng] = o.idx
        for k in writes:
            self.atoms[k] = [o.idx, {}, []]
        if dma:
            o.dma_n = len(self.dma_ops[eng])
            if o.dma_n >= NLANE:
                deps.add(self.dma_ops[eng][o.dma_n - NLANE].idx)
            self.dma_ops[eng].append(o)
        deps.discard(o.idx)
        if eng == "pe":
            deps = {d for d in deps if not (self.ops[d].eng == "pe" and not self.ops[d].is_dma)}
        o.deps = deps
        self.ops.append(o)
        self.by_eng[eng].append(o)
        return o

    def emit(self, nc, block):
        ops = self.ops
        for o in ops:
            for d in o.deps:
                if not ops[d].is_dma:
                    ops[d].signals = True
        for e in self.ENGS:
            c = 0
            for o in self.by_eng[e]:
                if not o.is_dma and o.signals:
                    c += 1
                o.count = c
        eng_sem = {e: nc.alloc_semaphore(name=f"sem_{e}") for e in self.ENGS}
        lane_sem = {e: [nc.alloc_semaphore(name=f"lane_{e}_{i}") for i in range(NLANE)]
                    for e in self.ENGS if self.dma_ops[e]}

        def make(e):
            def body(eng):
                known = {}
                for o in self.by_eng[e]:
                    need = {}
                    for d in o.deps:
                        dd = ops[d]
                        if dd.is_dma:
                            sem = lane_sem[dd.eng][dd.dma_n % NLANE]
                            val = 16 * (dd.dma_n // NLANE + 1)
                        else:
                            sem = eng_sem[dd.eng]
                            val = dd.count
                        key = sem.num
                        if need.get(key, (None, 0))[1] < val:
                            need[key] = (sem, val)
                    for key, (sem, val) in need.items():
                        if known.get(key, 0) < val:
                            eng.wait_ge(sem, val)
                            known[key] = val
                    ins = o.fn(eng)
                    if o.is_dma:
                        ins.then_inc(lane_sem[e][o.dma_n % NLANE], 16)
                    elif o.signals:
                        ins.then_inc(eng_sem[e], 1)
            return body

        block.tensor(make("pe"))
        block.scalar(make("act"))
        block.vector(make("dve"))
        block.gpsimd(make("pool"))
        block.sync(make("sp"))


def _sbk(lo, hi):
    return [("sb", a) for a in range(lo // ATOM, (hi - 1) // ATOM + 1)]


class Buf:
    def __init__(self, nc, name, free_shape, dtype, off, parts=128):
        self.t = nc.alloc_sbuf_tensor_at(name, [parts] + list(free_shape), dtype, offset=off)
        self.off = off
        self.es = 2 if dtype == BF16 else 4
        self.n = int(np.prod(free_shape))
        self.nbytes = self.n * self.es

    def k(self, lo=0, hi=None):
        hi = self.n if hi is None else hi
        return _sbk(self.off + lo * self.es, self.off + hi * self.es)


def build_nc(n_blocks=SEQ // T):
    nc = bass.Bass("TRN2", target_bir_lowering=False)
    S = n_blocks * T

    def din(name, shape, dt=F32):
        return nc.dram_tensor(name, list(shape), dt, kind="ExternalInput")

    x_h = din("x", [S, D])
    c_h = din("c", [D])
    w_ada_h = din("w_ada", [D, NMOD * D])
    b_ada_h = din("b_ada", [NMOD * D])
    gmix_h = din("norm_mix_g", [D])
    w_in_h = din("w_in", [D, DIN])
    w3_h = din("conv_short_w", [3, D])
    wA_h = din("w_short_out", [D, D])
    w31_h = din("conv_conf_w", [31, D])
    cb_h = din("conv_conf_b", [D])
    lng_h = din("conf_ln_g", [D])
    lnb_h = din("conf_ln_b", [D])
    wB_h = din("w_conf_out", [D, D])
    wo_h = din("w_o", [D, D])
    gffn_h = din("norm_ffn_g", [D])
    wfi_h = din("w_ffn_in", [D, 2 * DFF])
    wfo_h = din("w_ffn_out", [DFF, D])
    gfin_h = din("final_norm_g", [D])
    identf_h = din("ident_f", [128, 128])
    identb_h = din("ident_b", [128, 128], BF16)
    out_h = nc.dram_tensor("out", [S, D], F32, kind="ExternalOutput")

    def dscr(name, shape, dt=BF16):
        return nc.dram_tensor(name, list(shape), dt, kind="Internal")

    win_s = dscr("win_s", [D, DIN])
    wa_s = dscr("wa_s", [D, D])
    wb_s = dscr("wb_s", [D, D])
    wo_s = dscr("wo_s", [D, D])
    wfi_s = dscr("wfi_s", [D, 2 * DFF])
    wfo_s = dscr("wfo_s", [DFF, D])
    dg_s = dscr("dg_s", [KC, 128, 31 * 128])
    mod_d = dscr("mod_d", [NMOD * D], F32)

    x, out = x_h.ap(), out_h.ap()

    SC = Sched()

    cur = [16384 + 512]

    def alloc(name, free_shape, dtype, parts=128, at=None):
        es = 2 if dtype == BF16 else 4
        nb = int(np.prod(free_shape)) * es
        if at is None:
            off = cur[0]
            cur[0] = off + ((nb + ATOM - 1) // ATOM) * ATOM
        else:
            off = at
        assert off + nb <= 229376, (name, off, nb)
        return Buf(nc, name, free_shape, dtype, off, parts)

    ident_b = alloc("ident_b", [128], BF16)
    ident_f = alloc("ident_f", [128], F32)
    ones_b = alloc("ones_b", [128], BF16)
    ones_f = alloc("ones_f", [128], F32)
    NCOL = 384
    cols = alloc("cols", [NCOL], F32)
    C_GMIX, C_GFFN, C_CB, C_LNG, C_LNB, C_W3, C_W31, C_C = 0, 8, 16, 24, 32, 40, 64, 312
    modc = alloc("modc", [48], F32)
    gsh = alloc("gsh", [32], F32)
    csilu = alloc("csilu", [8], F32)
    epsb = alloc("epsb", [2], F32)
    nhalf = alloc("nhalf", [4], F32)
    GF = alloc("GF", [D], F32)
    d3 = alloc("d3", [KC, 3, 128], BF16)
    stats3 = [alloc(f"stat{i}", [16], F32) for i in range(3)]
    lnm = alloc("lnm", [T], F32)
    lnv = alloc("lnv", [T], F32)
    lnr = alloc("lnr", [T], F32)
    lnt = alloc("lnt", [T], F32)
    ring = [alloc(f"ring{i}", [SLOT_BYTES // 2], BF16) for i in range(NSLOT)]
    act_base = cur[0]
    xres = [alloc(f"xres{i}", [NT, D], F32) for i in range(2)]
    xs = alloc("xs", [NT, D], BF16)
    junk = alloc("junk", [D], BF16)
    hb = alloc("h", [KC, T], BF16)
    tmpp = [alloc(f"tmp{i}", [T], F32) for i in range(4)]
    CVW = 640
    cvb = alloc("cvb", [KC, CVW], BF16)
    t1 = alloc("t1", [KC, T], BF16)
    UW = 640
    ub_ = alloc("u", [KC, UW], BF16)
    big_off = cur[0]
    ya = alloc("ya", [KC, T], BF16)
    cvo = alloc("cvo", [KC, T], F32)
    merged = alloc("merged", [KC, T], BF16, at=big_off)
    actb = alloc("actb", [FC, T], BF16, at=big_off)
    sq = [alloc(f"sq{i}", [T], BF16) for i in range(4)]
    zt = [alloc(f"zt{i}", [T], F32) for i in range(2)]
    ubn = alloc("ubn", [KC, T], BF16)
    h2b = alloc("h2", [KC, T], BF16, at=t1.off)
    act_end = cur[0]
    so = act_base
    NWST = 3
    wada_st = [alloc(f"wada{i}", [KC, 512], F32, at=so + i * 16384) for i in range(NWST)]
    modrow = alloc("modrow", [NMOD * D], F32, parts=1, at=so + 49152)
    brow = alloc("brow", [NMOD * D], F32, parts=1, at=so + 49152 + 24576)
    rows = alloc("rows", [3, 128], F32, at=so + 98304)
    rows2 = alloc("rows2", [128], F32, at=so + 98304 + 2048)
    dstage = [alloc(f"dstage{i}", [31, 128], BF16, at=so + 102400 + i * 8192) for i in range(2)]
    assert so + 102400 + 16384 <= act_end
    ringf = [alloc(f"ringf{i}", [SLOT_BYTES // 4], F32, at=ring[i].off) for i in range(NSLOT)]

    ps = nc.alloc_psum_tensor("ps", [128, 4096], F32)

    def psk(bank, half=None):
        if half is None:
            return [("ps", 2 * bank), ("ps", 2 * bank + 1)]
        return [("ps", 2 * bank + half)]

    def psb(bank):
        return ps[:, bank * 512:(bank + 1) * 512]

    def psb_bf(bank, half):
        return ps[:, bank * 512 + half * 256: bank * 512 + (half + 1) * 256].bitcast(BF16)

    bank_rr = [0]

    nrot = [6]

    def next_bank():
        b = bank_rr[0] % nrot[0]
        bank_rr[0] = (b + 1) % nrot[0]
        return b

    S1B, S2B = 6, 7
    tmp_rr = [0]

    def next_tmp():
        i = tmp_rr[0]
        tmp_rr[0] = (i + 1) % 4
        return tmpp[i]

    def dma(eng, out_ap, in_ap, reads, writes, **kw):
        def fn(e, out_ap=out_ap, in_ap=in_ap, kw=kw):
            return e.dma_start(out=out_ap, in_=in_ap, **kw)
        return SC.op(eng, fn, reads, writes, dma=True)

    def mm_group(out_ap, pairs, reads, writes):
        def fn(pe, out_ap=out_ap, pairs=pairs):
            n = len(pairs)
            ins = None
            for i, (l, r) in enumerate(pairs):
                ins = pe.matmul(out_ap, l, r, start=(i == 0), stop=(i == n - 1))
            return ins
        return SC.op("pe", fn, reads, writes)

    def act_op(out_ap, in_ap, func, reads, writes, scale=None, bias=None, accum=None):
        def fn(a, out_ap=out_ap, in_ap=in_ap, func=func, scale=scale, bias=bias, accum=accum):
            kw = {}
            if scale is not None:
                kw["scale"] = scale
            if bias is not None:
                kw["bias"] = bias
            if accum is not None:
                kw["accum_out"] = accum
            return a.activation(out=out_ap, in_=in_ap, func=func, **kw)
        return SC.op("act", fn, reads, writes)

    def tt(eng, out_ap, in0, in1, op, reads, writes):
        def fn(v, out_ap=out_ap, in0=in0, in1=in1, op=op):
            return v.tensor_tensor(out=out_ap, in0=in0, in1=in1, op=op)
        return SC.op(eng, fn, reads, writes)

    def ts(eng, out_ap, in0, s1, op0, reads, writes, s2=None, op1=None):
        def fn(v, out_ap=out_ap, in0=in0, s1=s1, s2=s2, op0=op0, op1=op1):
            if op1 is None:
                return v.tensor_scalar(out=out_ap, in0=in0, scalar1=s1, scalar2=None, op0=op0)
            return v.tensor_scalar(out=out_ap, in0=in0, scalar1=s1, scalar2=s2, op0=op0, op1=op1)
        return SC.op(eng, fn, reads, writes)

    def stt(out_ap, in0, scalar, in1, op0, op1, reads, writes):
        def fn(v, out_ap=out_ap, in0=in0, scalar=scalar, in1=in1, op0=op0, op1=op1):
            return v.scalar_tensor_tensor(out=out_ap, in0=in0, scalar=scalar, in1=in1, op0=op0, op1=op1)
        return SC.op("dve", fn, reads, writes)

    dma("sp", ident_f.t[:], identf_h.ap(), [], ident_f.k())
    dma("sp", ident_b.t[:], identb_h.ap(), [], ident_b.k())
    SC.op("dve", lambda v: v.memset(ones_b.t[:], 1.0), [], ones_b.k())
    SC.op("dve", lambda v: v.memset(ones_f.t[:], 1.0), [], ones_f.k())
    SC.op("dve", lambda v: v.memset(nhalf.t[:], -0.5), [], nhalf.k())
    SC.op("dve", lambda v: v.memset(epsb.t[:, 0:1], EPS), [], epsb.k())
    SC.op("dve", lambda v: v.memset(epsb.t[:, 1:2], LN_EPS), epsb.k(), epsb.k())

    SC.op("dve", lambda v: v.memset(rows.t[:], 0.0), [], rows.k())

    rowkeys = []

    def rowvec(h, off, nrow=KC, base=0):
        done = 0
        while done < nrow:
            r = off + done
            g, p0 = r // 128, r % 128
            n = min(nrow - done, 128 - p0)
            src = bass.AP(h, base + done * 128, [[128, n], [1, 128]])
            rowkeys.append(("rows", g, p0))
            dma("sp", rows.t[p0:p0 + n, g, :], src, rows.k(), [("rows", g, p0)])
            done += n

    rowvec(c_h, C_C)
    rowvec(gmix_h, C_GMIX)
    rowvec(gffn_h, C_GFFN)
    rowvec(cb_h, C_CB)
    rowvec(lng_h, C_LNG)
    rowvec(lnb_h, C_LNB)
    rowvec(w3_h, C_W3, nrow=3 * KC)
    rowvec(w31_h, C_W31, nrow=31 * KC)
    tb = next_bank()

    def _tr_rows(pe, tb=tb):
        ins = None
        for g in range(3):
            ins = pe.transpose(ps[:, tb * 512 + g * 128: tb * 512 + (g + 1) * 128], rows.t[:, g, :], ident_f.t[:])
        return ins
    SC.op("pe", _tr_rows, rows.k() + rowkeys + ident_f.k(), psk(tb))
    SC.op("dve", lambda v, tb=tb: v.tensor_copy(out=cols.t[:], in_=ps[:, tb * 512: tb * 512 + NCOL]),
          psk(tb), cols.k())

    dma("sp", GF.t[:], bass.AP(gfin_h, 0, [[0, 128], [1, D]]), [], GF.k())

    act_op(csilu.t[:], cols.t[:, C_C:C_C + KC], AF.Silu, cols.k(C_C, C_C + KC), csilu.k())
    for j in range(KC):
        for i in range(3):
            ts("dve", d3.t[:, j, i, :], ident_f.t[:], cols.t[:, C_W3 + i * KC + j:C_W3 + i * KC + j + 1], ALU.mult,
               ident_f.k() + cols.k(), d3.k((j * 3 + i) * 128, (j * 3 + i + 1) * 128))

    def npe_of(b, j):
        if b == 0:
            return 24 if j < 7 else 31
        if j < 4:
            return 0
        return (14, 14, 24, 31)[j - 4]

    def npe_max(j):
        return max(npe_of(0, j), npe_of(1, j))

    k_dg = []
    for j in range(KC):
        ds_ = dstage[j % 2]
        n = npe_max(j)
        for i in range(n):
            ts("dve", ds_.t[:, i, :], ident_f.t[:], cols.t[:, C_W31 + i * KC + j:C_W31 + i * KC + j + 1], ALU.mult,
               ident_f.k() + cols.k(), ds_.k(i * 128, (i + 1) * 128))
        key = ("dr", "dg", j)
        k_dg.append(key)
        dma("act", dg_s.ap()[j][:, 0:n * 128], ds_.t[:, 0:n, :].rearrange("p i c -> p (i c)"), ds_.k(0, n * 128), [key])

    dma("sp", brow.t[:], bass.AP(b_ada_h, 0, [[0, 1], [1, NMOD * D]]), [], brow.k())
    def wada_load(g):
        st = wada_st[g % NWST]
        src = w_ada_h.ap()[:, g * 512:(g + 1) * 512].rearrange("(k p) c -> p k c", p=128)
        dma("sp", st.t[:], src, [], st.k())

    for g in range(NWST):
        wada_load(g)
    for g in range(NMOD * D // 512):
        st = wada_st[g % NWST]
        bank = next_bank()
        pairs = [(csilu.t[:, kc:kc + 1], st.t[:, kc, :]) for kc in range(KC)]
        mm_group(ps[0:1, bank * 512:(bank + 1) * 512], pairs, csilu.k() + st.k(), psk(bank))
        tt("dve", modrow.t[:, g * 512:(g + 1) * 512], ps[0:1, bank * 512:(bank + 1) * 512],
           brow.t[:, g * 512:(g + 1) * 512], ALU.add,
           psk(bank) + brow.k(g * 512, (g + 1) * 512), modrow.k(g * 512, (g + 1) * 512))
        if g + NWST < NMOD * D // 512:
            wada_load(g + NWST)
    kmod = [("dr", "mod")]
    dma("sp", bass.AP(mod_d, 0, [[0, 1], [1, NMOD * D]]), modrow.t[:], modrow.k(), kmod)
    SC.op("dve", lambda v: v.memset(rows2.t[:], 0.0), [], rows2.k())
    dma("sp", rows2.t[0:NMOD * KC, :], bass.AP(mod_d, 0, [[128, NMOD * KC], [1, 128]]), kmod + rows2.k(), rows2.k())
    tb2 = next_bank()
    SC.op("pe", lambda pe, tb2=tb2: pe.transpose(ps[:, tb2 * 512: tb2 * 512 + 128], rows2.t[:], ident_f.t[:]),
          rows2.k() + ident_f.k(), psk(tb2))
    SC.op("dve", lambda v, tb2=tb2: v.tensor_copy(out=modc.t[:], in_=ps[:, tb2 * 512: tb2 * 512 + NMOD * KC]),
          psk(tb2), modc.k())
    M_SH1, M_SC1, M_SH2, M_SC2 = 0, 8, 24, 32
    stt(gsh.t[:, 0:8], modc.t[:, M_SC1:M_SC1 + 8], 1.0, cols.t[:, C_GMIX:C_GMIX + 8], ALU.add, ALU.mult,
        modc.k() + cols.k(), gsh.k())
    stt(gsh.t[:, 8:16], modc.t[:, M_SC2:M_SC2 + 8], 1.0, cols.t[:, C_GFFN:C_GFFN + 8], ALU.add, ALU.mult,
        modc.k() + cols.k(), gsh.k())

    def w_item(sh, oh, c0, ncol, r0=0, kcn=KC):
        sl = lambda h: h.ap()[r0:r0 + kcn * 128, c0:c0 + ncol].rearrange("(k p) c -> p k c", p=128)
        return dict(kind="w", scr=sl(sh), src32=sl(oh), kcn=kcn, ncol=ncol, key=("dr", sh.name, c0, r0))

    def dg_item(b, j):
        n = npe_of(b, j)
        src = dg_s.ap()[j].rearrange("p (i c) -> p i c", c=128)[:, 0:n, :]
        return dict(kind="dg", scr=src, kcn=n, ncol=128, key=k_dg[j], tag=("dg", j))

    def g_item(q):
        return dict(kind="g", scr=bass.AP(mod_d, q * D, [[0, 128], [1, D]]), kcn=1, ncol=D, key=kmod[0], tag=("g", q))

    def win_item(tag, c0):
        it_ = w_item(win_s, w_in_h, c0, 512)
        it_["tag"] = tag
        return it_

    def glu_items(jj):
        return [win_item(("vc", jj), 3072 + jj * 512), win_item(("gl", jj), 4096 + jj * 512)]

    def conv_sched(b):
        sch = [[] for _ in range(KC)]
        if b == 0:
            for j in range(KC):
                n0 = npe_of(0, j)
                sch[j] = [("pe", j)] + [("tap", j, i) for i in range(n0, 31)] + [("fin", j)]
            return sch
        n4, n5, n6 = npe_of(b, 4), npe_of(b, 5), npe_of(b, 6)
        t45 = []
        for i in range(min(n4, n5), 31):
            if i >= n4:
                t45.append(("tap", 4, i))
            if i >= n5:
                t45.append(("tap", 5, i))
        q = (len(t45) + 3) // 4
        sch[1] = [("pe", 4), ("pe", 5)] + t45[0:q - 2]
        sch[2] = t45[q - 2:2 * q - 2]
        sch[3] = t45[2 * q - 2:3 * q - 2]
        sch[4] = t45[3 * q - 2:] + [("fin", 4), ("fin", 5)]
        n7 = npe_of(b, 7)
        t6 = []
        for i in range(min(n6, n7), 31):
            if i >= n6:
                t6.append(("tap", 6, i))
            if i >= n7:
                t6.append(("tap", 7, i))
        h = (len(t6) + 2) // 3
        sch[5] = [("pe", 6), ("pe", 7)] + t6[0:h]
        sch[6] = t6[h:2 * h]
        sch[7] = t6[2 * h:] + [("fin", 6), ("fin", 7)]
        return sch

    def block_items(b):
        it = []
        if b == 0:
            for jj in range(2):
                it += glu_items(jj)
        sch = conv_sched(b)
        for j in range(KC):
            if j % 4 == 0:
                jj = j // 4
                it.append(win_item(("C", jj), 1024 + jj * 512))
                it.append(win_item(("V", jj), 2048 + jj * 512))
                it.append(win_item(("B", jj), 0 + jj * 512))
            for a in sch[j]:
                if a[0] == "pe" and npe_of(b, a[1]) > 0:
                    it.append(dg_item(b, a[1]))
        for mm in range(2):
            it.append(win_item(("ga", mm), 5120 + mm * 512))
            w = w_item(wa_s, wA_h, mm * 512, 512); w["tag"] = ("wa", mm); it.append(w)
        for mm in range(2):
            it.append(win_item(("gb", mm), 6144 + mm * 512))
            w = w_item(wb_s, wB_h, mm * 512, 512); w["tag"] = ("wb", mm); it.append(w)
        if b == 0:
            it.append(g_item(2))
        for half in range(2):
            w = w_item(wo_s, wo_h, half * 512, 512); w["tag"] = ("wo", half); it.append(w)
        if b + 1 < n_blocks:
            for jj in range(2):
                it += glu_items(jj)
        for jj in range(6):
            ncol = 512 if jj < 5 else 256
            w = w_item(wfi_s, wfi_h, jj * 512, ncol); w["tag"] = ("fa", jj); it.append(w)
            w = w_item(wfi_s, wfi_h, DFF + jj * 512, ncol); w["tag"] = ("fb", jj); it.append(w)
        for half in range(2):
            if half == 0 and b == 0:
                it.append(g_item(5))
            for g, kcn in enumerate((8, 8, 6)):
                w = w_item(wfo_s, wfo_h, half * 512, 512, r0=g * 1024, kcn=kcn); w["tag"] = ("fo", half, g); it.append(w)
        return it

    items = []
    first_use = {}
    for b in range(n_blocks):
        for itm in block_items(b):
            itm = dict(itm)
            itm["first"] = itm["kind"] == "w" and itm["key"] not in first_use
            if itm["first"]:
                first_use[itm["key"]] = True
            items.append(itm)
    n_items = len(items)
    issued = [0]
    taken = [0]
    free_slots = list(range(NSLOT))
    item_slot = {}

    def slot_view(n):
        itm = items[n]
        s_ = item_slot[n]
        if itm["kind"] == "g":
            return ringf[s_], ringf[s_].t[:, 0:D], ringf[s_].k(0, D)
        kcn, ncol = itm["kcn"], itm["ncol"]
        r = ring[s_]
        return r, r.t[:, 0:kcn * ncol].rearrange("p (k c) -> p k c", c=ncol), r.k(0, kcn * ncol)

    def issue_items():
        while issued[0] < n_items and free_slots:
            n = issued[0]
            issued[0] += 1
            item_slot[n] = free_slots.pop(0)
            itm = items[n]
            r, v, keys = slot_view(n)
            if itm["first"]:
                dma("pool", v, itm["src32"], [], keys)
                if itm["tag"][0] not in ("wo", "fo"):
                    dma("sp", itm["scr"], v, keys, [itm["key"]])
            else:
                dma("sp", v, itm["scr"], [itm["key"]], keys)

    def take_item(tag):
        n = taken[0]
        taken[0] += 1
        assert n < issued[0], "weight ring too small for this consumption order"
        assert items[n]["tag"] == tag, (items[n]["tag"], tag)
        r, v, keys = slot_view(n)
        itm = items[n]
        if itm["first"] and itm["tag"][0] in ("wo", "fo"):
            gq = cur_gate[0]
            half = itm["tag"][1]
            for kc in range(itm["kcn"]):
                tt("dve", v[:, kc, :], v[:, kc, :], gq[2][:, half * 512:(half + 1) * 512], ALU.mult,
                   keys + gq[1].k(0, D), keys)
            dma("sp", itm["scr"], v, keys, [itm["key"]])
        return n, r, v

    cur_gate = [None]

    def release_item(n):
        free_slots.append(item_slot[n])
        issue_items()

    issue_items()

    def load_x(b):
        xr = xres[b % 2]
        src = x[b * T:(b + 1) * T, :].rearrange("(i p) d -> p i d", p=128)
        dma("sp", xr.t[:], src, [], xr.k())

    def psb_bf1k(bank):
        return ps[:, bank * 512:(bank + 1) * 512].bitcast(BF16)

    def norm_stats(xr, st, tiles):
        for i in tiles:
            act_op(junk.t[:], xr.t[:, i, :], AF.Square, xr.k(i * D, (i + 1) * D), st.k(i, i + 1),
                   accum=st.t[:, i:i + 1])
        lo, hi = tiles[0], tiles[-1] + 1
        ts("pool", st.t[:, 4 + lo:4 + hi], st.t[:, lo:hi], 1.0 / D, ALU.mult, st.k(), st.k(), s2=EPS, op1=ALU.add)
        tt("pool", st.t[:, 8 + lo:8 + hi], st.t[:, 4 + lo:4 + hi], nhalf.t[:, 0:hi - lo], ALU.pow,
           st.k() + nhalf.k(), st.k())
        for i in tiles:
            ts("dve", xs.t[:, i, :], xr.t[:, i, :], st.t[:, 8 + i:9 + i], ALU.mult,
               xr.k(i * D, (i + 1) * D) + st.k(), xs.k(i * D, (i + 1) * D))

    def norm_transpose(i, hdst, gs_off, sh_off):
        bank = next_bank()

        def fn(pe, i=i, bank=bank):
            ins = None
            o = psb_bf1k(bank)
            for k in range(KC):
                ins = pe.transpose(o[:, k * 128:(k + 1) * 128], xs.t[:, i, k * 128:(k + 1) * 128], ident_b.t[:])
            return ins
        SC.op("pe", fn, xs.k(i * D, (i + 1) * D) + ident_b.k(), psk(bank))
        for k in range(KC):
            act_op(hdst.t[:, k, i * 128:(i + 1) * 128], psb_bf1k(bank)[:, k * 128:(k + 1) * 128], AF.Identity,
                   psk(bank) + gsh.k() + modc.k(), hdst.k(k * T + i * 128, k * T + (i + 1) * 128),
                   scale=gsh.t[:, gs_off + k:gs_off + k + 1], bias=modc.t[:, sh_off + k:sh_off + k + 1])

    def proj_group(slot_r, slot_v, jl, rhs_buf):
        bank = next_bank()
        pairs = [(slot_v[:, kc, jl * 128:(jl + 1) * 128], rhs_buf.t[:, kc, :]) for kc in range(KC)]
        mm_group(psb(bank), pairs, slot_r.k() + rhs_buf.k(), psk(bank))
        return bank

    def glu_chunk(j, iVC, iGL):
        jl = j % 4
        bG = proj_group(iGL[1], iGL[2], jl, hb)
        tG = next_tmp()
        act_op(tG.t[:], psb(bG), AF.Sigmoid, psk(bG), tG.k())
        bV = proj_group(iVC[1], iVC[2], jl, hb)
        tt("dve", ub_.t[:, j, 30:30 + T], psb(bV), tG.t[:], ALU.mult,
           psk(bV) + tG.k(), ub_.k(j * UW, (j + 1) * UW))

    stat_n = [0]

    def stats(j):
        sqj = sq[j % 4]
        first, lastc = stat_n[0] % KC == 0, stat_n[0] % KC == KC - 1
        stat_n[0] += 1

        def fn(pe, j=j, sqj=sqj, first=first, lastc=lastc):
            pe.matmul(psb(S1B), ones_f.t[:], cvo.t[:, j, :], start=first, stop=lastc)
            return pe.matmul(psb(S2B), ones_b.t[:], sqj.t[:], start=first, stop=lastc)
        SC.op("pe", fn, ones_f.k() + ones_b.k() + cvo.k(j * T, (j + 1) * T) + sqj.k(), psk(S1B) + psk(S2B))

    def conv_tap(j, i, bank):
        sc = cols.t[:, C_W31 + i * KC + j:C_W31 + i * KC + j + 1]
        src = ub_.t[:, j, i:i + T]
        rk = ub_.k(j * UW, (j + 1) * UW) + cols.k() + psk(bank)
        if i == 0:
            ts("dve", psb(bank), src, sc, ALU.mult, ub_.k(j * UW, (j + 1) * UW) + cols.k(), psk(bank))
        else:
            stt(psb(bank), src, sc, psb(bank), ALU.mult, ALU.add, rk, psk(bank))

    def conv_fin(j, bank):
        ukeys = ub_.k(j * UW, (j + 1) * UW)
        act_op(cvo.t[:, j, :], psb(bank), AF.Identity, psk(bank) + cols.k(), cvo.k(j * T, (j + 1) * T),
               bias=cols.t[:, C_CB + j:C_CB + j + 1])
        sqj = sq[j % 4]
        act_op(sqj.t[:], cvo.t[:, j, :], AF.Square, cvo.k(j * T, (j + 1) * T), sqj.k())
        SC.op("dve", lambda v, j=j: v.tensor_copy(out=ub_.t[:, j, 0:30], in_=ub_.t[:, j, T:T + 30]),
              ukeys, ukeys)

    prev_final = [None]
    need_x = [False]

    def do_block(b):
        xr = xres[b % 2]
        last = b + 1 >= n_blocks
        if b == 0:
            load_x(0)
            norm_stats(xr, stats3[0], list(range(NT)))
            for i in range(NT):
                norm_transpose(i, hb, 0, M_SH1)
            for jj in range(2):
                iVC = take_item(("vc", jj))
                iGL = take_item(("gl", jj))
                for jl in range(4):
                    glu_chunk(jj * 4 + jl, iVC, iGL)
                release_item(iVC[0])
                release_item(iGL[0])
        if not last and prev_final[0] is None:
            load_x(b + 1)

        nrot[0] = 4
        stat_q = []
        if b > 0:
            for cj in range(4):
                conv_fin(cj, 4 + cj)
            stat_q.extend(range(4))
        sch = conv_sched(b)
        cbank = {}
        tB_of = {}

        def conv3(j):
            bank = next_bank()
            ck = cvb.k(j * CVW, (j + 1) * CVW)
            pairs = [(d3.t[:, j, i, :], cvb.t[:, j, i:i + T]) for i in range(3)]
            mm_group(psb(bank), pairs, d3.k() + ck, psk(bank))
            tB = tB_of.pop(j)
            tt("dve", ya.t[:, j, :], psb(bank), tB.t[:], ALU.mult, psk(bank) + tB.k(), ya.k(j * T, (j + 1) * T))
            SC.op("dve", lambda v, j=j: v.tensor_copy(out=cvb.t[:, j, 0:2], in_=cvb.t[:, j, T:T + 2]), ck, ck)

        def conv_unit(a):
            cj = a[1]
            n0 = npe_of(b, cj)
            if a[0] == "pe":
                bank = 4 + cj % 2
                cbank[cj] = bank
                if n0 > 0:
                    nD, rD, vD = take_item(("dg", cj))
                    pairs = [(vD[:, i, :], ub_.t[:, cj, i:i + T]) for i in range(n0)]
                    mm_group(psb(bank), pairs, rD.k() + ub_.k(cj * UW, (cj + 1) * UW), psk(bank))
                    release_item(nD)
            elif a[0] == "fin":
                conv_fin(cj, cbank.pop(cj))
                stat_q.append(cj)
            else:
                conv_tap(cj, a[2], cbank[cj])

        def conv_units(units, n):
            for _ in range(n):
                if units:
                    conv_unit(units.pop(0))

        for j in range(KC):
            jl = j % 4
            units = list(sch[j])
            if jl == 0:
                iC = take_item(("C", j // 4))
                iV = take_item(("V", j // 4))
                iB = take_item(("B", j // 4))
            per = (len(units) + 3) // 4
            conv_units(units, per)
            bC = proj_group(iC[1], iC[2], jl, hb)
            tC = next_tmp()
            act_op(tC.t[:], psb(bC), AF.Copy, psk(bC), tC.k())
            conv_units(units, per)
            bVv = proj_group(iV[1], iV[2], jl, hb)
            tt("dve", cvb.t[:, j, 2:2 + T], psb(bVv), tC.t[:], ALU.mult,
               psk(bVv) + tC.k(), cvb.k(j * CVW, (j + 1) * CVW))
            conv_units(units, per)
            bB = proj_group(iB[1], iB[2], jl, hb)
            tB = next_tmp()
            act_op(tB.t[:], psb(bB), AF.Copy, psk(bB), tB.k())
            tB_of[j] = tB
            if jl == 3:
                release_item(iC[0])
                release_item(iV[0])
                release_item(iB[0])
            if j >= 1:
                conv3(j - 1)
            conv_units(units, len(units))
            if j == 1 and prev_final[0] is not None:
                prev_final[0]()
                prev_final[0] = None
                need_x[0] = not last
            if j >= 1:
                for _ in range(2):
                    if stat_q:
                        stats(stat_q.pop(0))
        conv3(KC - 1)
        while len(stat_q) > 2:
            stats(stat_q.pop(0))
        nrot[0] = 6

        def ln_finalize():
            ts("dve", lnm.t[:], psb(S1B), 1.0 / D, ALU.mult, psk(S1B), lnm.k())
            tt("dve", lnt.t[:], lnm.t[:], lnm.t[:], ALU.mult, lnm.k(), lnt.k())
            stt(lnv.t[:], psb(S2B), 1.0 / D, lnt.t[:], ALU.mult, ALU.subtract, psk(S2B) + lnt.k(), lnv.k())
            act_op(lnv.t[:], lnv.t[:], AF.Sqrt, lnv.k() + epsb.k(), lnv.k(), bias=epsb.t[:, 1:2])
            SC.op("dve", lambda v: v.reciprocal(out=lnv.t[:], in_=lnv.t[:]), lnv.k(), lnv.k())
            tt("dve", lnr.t[:], lnm.t[:], lnv.t[:], ALU.mult, lnm.k() + lnv.k(), lnr.k())

        def ln_z(j):
            ck = cvo.k(j * T, (j + 1) * T)
            tt("dve", cvo.t[:, j, :], cvo.t[:, j, :], lnv.t[:], ALU.mult, ck + lnv.k(), ck)
            tt("dve", cvo.t[:, j, :], cvo.t[:, j, :], lnr.t[:], ALU.subtract, ck + lnr.k(), ck)

        def ln_silu(j):
            act_op(ubn.t[:, j, :], cvo.t[:, j, :], AF.Silu, cvo.k(j * T, (j + 1) * T) + cols.k(),
                   ubn.k(j * T, (j + 1) * T),
                   scale=cols.t[:, C_LNG + j:C_LNG + j + 1], bias=cols.t[:, C_LNB + j:C_LNB + j + 1])

        if need_x[0]:
            load_x(b + 1)
            need_x[0] = False
        for mm in range(2):
            iG = take_item(("ga", mm))
            iW = take_item(("wa", mm))
            for ml in range(4):
                m = mm * 4 + ml
                bG = proj_group(iG[1], iG[2], ml, hb)
                tG = next_tmp()
                act_op(tG.t[:], psb(bG), AF.Sigmoid, psk(bG), tG.k())
                bY = proj_group(iW[1], iW[2], ml, ya)
                tt("dve", t1.t[:, m, :], psb(bY), tG.t[:], ALU.mult, psk(bY) + tG.k(), t1.k(m * T, (m + 1) * T))
                if m == 1:
                    while stat_q:
                        stats(stat_q.pop(0))
                    ln_finalize()
                if 2 <= m <= 5:
                    ln_z(2 * (m - 2))
                    ln_z(2 * (m - 2) + 1)
                if m == 5:
                    for j in range(KC):
                        ln_silu(j)
            release_item(iG[0])
            release_item(iW[0])

        nxt = xres[(b + 1) % 2]
        for mm in range(2):
            iG = take_item(("gb", mm))
            iW = take_item(("wb", mm))
            if mm == 1 and not last:
                norm_stats(nxt, stats3[0], list(range(NT)))
            for ml in range(4):
                m = mm * 4 + ml
                bG = proj_group(iG[1], iG[2], ml, hb)
                tG = next_tmp()
                act_op(tG.t[:], psb(bG), AF.Sigmoid, psk(bG), tG.k())
                bY = proj_group(iW[1], iW[2], ml, ubn)
                z = zt[m % 2]
                tt("dve", z.t[:], psb(bY), tG.t[:], ALU.mult, psk(bY) + tG.k(), z.k())
                tt("dve", merged.t[:, m, :], z.t[:], t1.t[:, m, :], ALU.add,
                   z.k() + t1.k(m * T, (m + 1) * T), merged.k(m * T, (m + 1) * T))
            release_item(iG[0])
            release_item(iW[0])

        pre = []
        if not last:
            for i in range(31):
                for j in range(4):
                    pre.append((j, i))

        def pre_taps(n):
            for _ in range(n):
                if pre:
                    j, i = pre.pop(0)
                    conv_tap(j, i, 4 + j)

        nrot[0] = 4 if not last else 6
        iG1 = None
        if b == 0:
            iG1 = take_item(("g", 2))
            cur_gate[0] = iG1
        iW0 = take_item(("wo", 0))
        iW1 = take_item(("wo", 1))
        st2 = stats3[1]

        def gated_add(bank, iG, i, half):
            lo, hi = i * D + half * 512, i * D + (half + 1) * 512
            tt("dve", xr.t[:, i, half * 512:(half + 1) * 512], psb(bank), xr.t[:, i, half * 512:(half + 1) * 512],
               ALU.add, psk(bank) + xr.k(lo, hi), xr.k(lo, hi))

        def wo_tile(i):
            for half, iW in enumerate((iW0, iW1)):
                bank = next_bank()
                pairs = [(merged.t[:, kc, i * 128:(i + 1) * 128], iW[2][:, kc, :]) for kc in range(KC)]
                mm_group(psb(bank), pairs, merged.k() + iW[1].k(), psk(bank))
                gated_add(bank, iG1, i, half)
            norm_stats(xr, st2, [i])

        if last:
            for i in range(NT):
                wo_tile(i)
                if i >= 1:
                    norm_transpose(i - 1, h2b, 8, M_SH2)
            norm_transpose(NT - 1, h2b, 8, M_SH2)
        else:
            for i in range(NT):
                norm_transpose(i, hb, 0, M_SH1)
                wo_tile(i)
            release_item(iW0[0])
            release_item(iW1[0])
            glu_it = {}
            for i in range(NT):
                jj = i // 2
                if i % 2 == 0:
                    glu_it[jj] = (take_item(("vc", jj)), take_item(("gl", jj)))
                iVC, iGL = glu_it[jj]
                glu_chunk(2 * i, iVC, iGL)
                if i == NT - 1:
                    norm_transpose(i, h2b, 8, M_SH2)
                    glu_chunk(2 * i + 1, iVC, iGL)
                else:
                    glu_chunk(2 * i + 1, iVC, iGL)
                    norm_transpose(i, h2b, 8, M_SH2)
                if i >= 1:
                    pre_taps(8)
                if i % 2 == 1:
                    release_item(iVC[0])
                    release_item(iGL[0])
        if iG1 is not None:
            release_item(iG1[0])
        if last:
            release_item(iW0[0])
            release_item(iW1[0])

        for jj in range(6):
            iA = take_item(("fa", jj))
            iB = take_item(("fb", jj))
            for jl in range(4 if jj < 5 else 2):
                j = jj * 4 + jl
                bA = proj_group(iA[1], iA[2], jl, h2b)
                tA = next_tmp()
                act_op(tA.t[:], psb(bA), AF.Silu, psk(bA), tA.k())
                bB = proj_group(iB[1], iB[2], jl, h2b)
                tt("dve", actb.t[:, j, :], psb(bB), tA.t[:], ALU.mult, psk(bB) + tA.k(), actb.k(j * T, (j + 1) * T))
                pre_taps(4)
            release_item(iA[0])
            release_item(iB[0])

        for half in range(2):
            if half == 0:
                iG2 = None
                if b == 0:
                    iG2 = take_item(("g", 5))
                    cur_gate[0] = iG2
            its = [take_item(("fo", half, g)) for g in range(3)]
            for i in range(NT):
                bank = next_bank()
                pairs = []
                rk = []
                for g, (n_, r_, v_) in enumerate(its):
                    rk += r_.k()
                    for kk in range(8 if g < 2 else 6):
                        kc = g * 8 + kk
                        pairs.append((actb.t[:, kc, i * 128:(i + 1) * 128], v_[:, kk, :]))
                mm_group(psb(bank), pairs, actb.k() + rk, psk(bank))
                gated_add(bank, iG2, i, half)
                pre_taps(5)
            for it_ in its:
                release_item(it_[0])
        if iG2 is not None:
            release_item(iG2[0])
        pre_taps(len(pre))

        def final_norm():
            st3 = stats3[2]
            for i in range(NT):
                act_op(junk.t[:], xr.t[:, i, :], AF.Square, xr.k(i * D, (i + 1) * D), st3.k(),
                       accum=st3.t[:, i:i + 1])
            ts("pool", st3.t[:, 4:8], st3.t[:, 0:4], 1.0 / D, ALU.mult, st3.k(), st3.k(), s2=EPS, op1=ALU.add)
            tt("pool", st3.t[:, 8:12], st3.t[:, 4:8], nhalf.t[:, 0:4], ALU.pow, st3.k() + nhalf.k(), st3.k())
            for i in range(NT):
                ts("pool", xr.t[:, i, :], xr.t[:, i, :], st3.t[:, 8 + i:9 + i], ALU.mult,
                   xr.k(i * D, (i + 1) * D) + st3.k(), xr.k(i * D, (i + 1) * D), s2=1.0, op1=ALU.mult)
                tt("pool", xr.t[:, i, :], xr.t[:, i, :], GF.t[:], ALU.mult,
                   xr.k(i * D, (i + 1) * D) + GF.k(), xr.k(i * D, (i + 1) * D))
            dma("sp", out[b * T:(b + 1) * T, :].rearrange("(i p) d -> p i d", p=128), xr.t[:], xr.k(),
                [("dr", "out", b)])

        if last:
            final_norm()
        else:
            prev_final[0] = final_norm
        return None

        st3 = stats3[2]
        for i in range(NT):
            act_op(junk.t[:], xr.t[:, i, :], AF.Square, xr.k(i * D, (i + 1) * D), st3.k(),
                   accum=st3.t[:, i:i + 1])
        act_op(st3.t[:, 4:8], st3.t[:, 0:4], AF.Sqrt, st3.k() + epsb.k(), st3.k(), scale=1.0 / D, bias=epsb.t[:, 0:1])
        SC.op("dve", lambda v: v.reciprocal(out=st3.t[:, 8:12], in_=st3.t[:, 4:8]), st3.k(), st3.k())
        for i in range(NT):
            ts("pool", xr.t[:, i, :], xr.t[:, i, :], st3.t[:, 8 + i:9 + i], ALU.mult,
               xr.k(i * D, (i + 1) * D) + st3.k(), xr.k(i * D, (i + 1) * D), s2=1.0, op1=ALU.mult)
            tt("pool", xr.t[:, i, :], xr.t[:, i, :], GF.t[:], ALU.mult,
               xr.k(i * D, (i + 1) * D) + GF.k(), xr.k(i * D, (i + 1) * D))
        return dma("sp", out[b * T:(b + 1) * T, :].rearrange("(i p) d -> p i d", p=128), xr.t[:], xr.k(),
                   [("dr", "out", b)])

    SC.op("dve", lambda v: v.memset(cvb.t[:], 0.0), [], cvb.k())
    SC.op("dve", lambda v: v.memset(ub_.t[:], 0.0), [], ub_.k())
    last_out = None
    for b in range(n_blocks):
        last_out = do_block(b)

    final_reads = [("dr", "out", b) for b in range(n_blocks)]
    SC.op("sp", lambda e: e.nop(), final_reads, [])

    with nc.Block() as block:
        SC.emit(nc, block)
    return nc


_NC_CACHE = {}


def _get_nc(n_blocks):
    if n_blocks not in _NC_CACHE:
        _NC_CACHE[n_blocks] = build_nc(n_blocks)
    return _NC_CACHE[n_blocks]


def kernel(x, c, w_ada, b_ada, norm_mix_g, w_in, conv_short_w, w_short_out,
           conv_conf_w, conv_conf_b, conf_ln_g, conf_ln_b, w_conf_out, w_o,
           norm_ffn_g, w_ffn_in, w_ffn_out, final_norm_g, _n_blocks=None, _cores=None):
    f = lambda a: np.ascontiguousarray(np.asarray(a, dtype=np.float32))
    x = f(x)
    B, S, _ = x.shape
    n_blocks = S // T if _n_blocks is None else _n_blocks
    S_use = n_blocks * T
    cores = list(range(B)) if _cores is None else _cores
    shared = {
        "w_ada": f(w_ada)[0], "b_ada": f(b_ada)[0], "norm_mix_g": f(norm_mix_g)[0], "w_in": f(w_in)[0],
        "conv_short_w": f(conv_short_w)[0], "w_short_out": f(w_short_out)[0], "conv_conf_w": f(conv_conf_w)[0],
        "conv_conf_b": f(conv_conf_b)[0], "conf_ln_g": f(conf_ln_g)[0], "conf_ln_b": f(conf_ln_b)[0],
        "w_conf_out": f(w_conf_out)[0], "w_o": f(w_o)[0], "norm_ffn_g": f(norm_ffn_g)[0],
        "w_ffn_in": f(w_ffn_in)[0], "w_ffn_out": f(w_ffn_out)[0], "final_norm_g": f(final_norm_g),
        "ident_f": np.eye(128, dtype=np.float32),
        "ident_b": np.eye(128, dtype=np.float32).astype(ml_dtypes.bfloat16),
    }
    cc = f(c)
    in_maps = []
    for i in cores:
        m = dict(shared)
        m["x"] = np.ascontiguousarray(x[i, :S_use])
        m["c"] = np.ascontiguousarray(cc[i])
        in_maps.append(m)
    nc = _get_nc(n_blocks)
    res = run_bass_kernel_spmd(nc, in_maps, core_ids=list(range(len(cores))))
    outs = [np.asarray(r["out"], dtype=np.float32) for r in res.results]
    return np.stack(outs, axis=0)
```

```python
import numpy as np
import ml_dtypes
import concourse.bass as bass
import concourse.mybir as mybir
from concourse.bass_utils import run_bass_kernel_spmd

F32 = mybir.dt.float32
BF16 = mybir.dt.bfloat16
AF = mybir.ActivationFunctionType
ALU = mybir.AluOpType

D = 1024
DFF = 2816
DIN = 7168
NMOD = 6
SEQ = 8192
T = 512
NT = T // 128
KC = D // 128
FC = DFF // 128
EPS = 1e-6
LN_EPS = 1e-5
ATOM = 256
NSLOT = 7
NPE = 16
SLOT_BYTES = 8192
NLANE = 8


class _Op:
    __slots__ = ("eng", "fn", "deps", "idx", "is_dma", "signals", "dma_n", "count")


class Sched:
    ENGS = ("pe", "act", "dve", "pool", "sp")

    def __init__(self):
        self.ops = []
        self.by_eng = {e: [] for e in self.ENGS}
        self.atoms = {}
        self.dma_ops = {e: [] for e in self.ENGS}

    def op(self, eng, fn, reads=(), writes=(), dma=False):
        o = _Op()
        o.eng, o.fn, o.is_dma, o.signals, o.count, o.dma_n = eng, fn, dma, False, 0, -1
        o.idx = len(self.ops)
        deps = set()
        for k in reads:
            a = self.atoms.get(k)
            if a is not None and a[0] >= 0:
                deps.add(a[0])
        for k in writes:
            a = self.atoms.get(k)
            if a is not None:
                if a[0] >= 0:
                    deps.add(a[0])
                deps.update(a[1].values())
                deps.update(a[2])
        for k in reads:
            a = self.atoms.get(k)
            if a is None:
                a = self.atoms[k] = [-1, {}, []]
            if dma:
                a[2].append(o.idx)
            else:
                a[1][eng] = o.idx
        for k in writes:
            self.atoms[k] = [o.idx, {}, []]
        if dma:
            o.dma_n = len(self.dma_ops[eng])
            if o.dma_n >= NLANE:
                deps.add(self.dma_ops[eng][o.dma_n - NLANE].idx)
            self.dma_ops[eng].append(o)
        deps.discard(o.idx)
        if eng == "pe":
            deps = {d for d in deps if not (self.ops[d].eng == "pe" and not self.ops[d].is_dma)}
        o.deps = deps
        self.ops.append(o)
        self.by_eng[eng].append(o)
        return o

    def emit(self, nc, block):
        ops = self.ops
        for o in ops:
            for d in o.deps:
                if not ops[d].is_dma:
                    ops[d].signals = True
        for e in self.ENGS:
            c = 0
            for o in self.by_eng[e]:
                if not o.is_dma and o.signals:
                    c += 1
                o.count = c
        eng_sem = {e: nc.alloc_semaphore(name=f"sem_{e}") for e in self.ENGS}
        lane_sem = {e: [nc.alloc_semaphore(name=f"lane_{e}_{i}") for i in range(NLANE)]
                    for e in self.ENGS if self.dma_ops[e]}

        def make(e):
            def body(eng):
                known = {}
                for o in self.by_eng[e]:
                    need = {}
                    for d in o.deps:
                        dd = ops[d]
                        if dd.is_dma:
                            sem = lane_sem[dd.eng][dd.dma_n % NLANE]
                            val = 16 * (dd.dma_n // NLANE + 1)
                        else:
                            sem = eng_sem[dd.eng]
                            val = dd.count
                        key = sem.num
                        if need.get(key, (None, 0))[1] < val:
                            need[key] = (sem, val)
                    for key, (sem, val) in need.items():
                        if known.get(key, 0) < val:
                            eng.wait_ge(sem, val)
                            known[key] = val
                    ins = o.fn(eng)
                    if o.is_dma:
                        ins.then_inc(lane_sem[e][o.dma_n % NLANE], 16)
                    elif o.signals:
                        ins.then_inc(eng_sem[e], 1)
            return body

        block.tensor(make("pe"))
        block.scalar(make("act"))
        block.vector(make("dve"))
        block.gpsimd(make("pool"))
        block.sync(make("sp"))


def _sbk(lo, hi):
    return [("sb", a) for a in range(lo // ATOM, (hi - 1) // ATOM + 1)]


class Buf:
    def __init__(self, nc, name, free_shape, dtype, off, parts=128):
        self.t = nc.alloc_sbuf_tensor_at(name, [parts] + list(free_shape), dtype, offset=off)
        self.off = off
        self.es = 2 if dtype == BF16 else 4
        self.n = int(np.prod(free_shape))
        self.nbytes = self.n * self.es

    def k(self, lo=0, hi=None):
        hi = self.n if hi is None else hi
        return _sbk(self.off + lo * self.es, self.off + hi * self.es)


def build_nc(n_blocks=SEQ // T):
    nc = bass.Bass("TRN2", target_bir_lowering=False)
    S = n_blocks * T

    def din(name, shape, dt=F32):
        return nc.dram_tensor(name, list(shape), dt, kind="ExternalInput")

    x_h = din("x", [S, D])
    c_h = din("c", [D])
    w_ada_h = din("w_ada", [D, NMOD * D])
    b_ada_h = din("b_ada", [NMOD * D])
    gmix_h = din("norm_mix_g", [D])
    w_in_h = din("w_in", [D, DIN])
    w3_h = din("conv_short_w", [3, D])
    wA_h = din("w_short_out", [D, D])
    w31_h = din("conv_conf_w", [31, D])
    cb_h = din("conv_conf_b", [D])
    lng_h = din("conf_ln_g", [D])
    lnb_h = din("conf_ln_b", [D])
    wB_h = din("w_conf_out", [D, D])
    wo_h = din("w_o", [D, D])
    gffn_h = din("norm_ffn_g", [D])
    wfi_h = din("w_ffn_in", [D, 2 * DFF])
    wfo_h = din("w_ffn_out", [DFF, D])
    gfin_h = din("final_norm_g", [D])
    identf_h = din("ident_f", [128, 128])
    identb_h = din("ident_b", [128, 128], BF16)
    out_h = nc.dram_tensor("out", [S, D], F32, kind="ExternalOutput")

    def dscr(name, shape, dt=BF16):
        return nc.dram_tensor(name, list(shape), dt, kind="Internal")

    win_s = dscr("win_s", [D, DIN])
    wa_s = dscr("wa_s", [D, D])
    wb_s = dscr("wb_s", [D, D])
    wo_s = dscr("wo_s", [D, D])
    wfi_s = dscr("wfi_s", [D, 2 * DFF])
    wfo_s = dscr("wfo_s", [DFF, D])
    dg_s = dscr("dg_s", [KC, 128, 31 * 128])
    mod_d = dscr("mod_d", [NMOD * D], F32)

    x, out = x_h.ap(), out_h.ap()

    SC = Sched()

    cur = [16384 + 512]

    def alloc(name, free_shape, dtype, parts=128, at=None):
        es = 2 if dtype == BF16 else 4
        nb = int(np.prod(free_shape)) * es
        if at is None:
            off = cur[0]
            cur[0] = off + ((nb + ATOM - 1) // ATOM) * ATOM
        else:
            off = at
        assert off + nb <= 229376, (name, off, nb)
        return Buf(nc, name, free_shape, dtype, off, parts)

    ident_b = alloc("ident_b", [128], BF16)
    ident_f = alloc("ident_f", [128], F32)
    ones_b = alloc("ones_b", [128], BF16)
    ones_f = alloc("ones_f", [128], F32)
    NCOL = 384
    cols = alloc("cols", [NCOL], F32)
    C_GMIX, C_GFFN, C_CB, C_LNG, C_LNB, C_W3, C_W31, C_C = 0, 8, 16, 24, 32, 40, 64, 312
    modc = alloc("modc", [48], F32)
    gsh = alloc("gsh", [32], F32)
    csilu = alloc("csilu", [8], F32)
    epsb = alloc("epsb", [2], F32)
    nhalf = alloc("nhalf", [4], F32)
    GF = alloc("GF", [D], F32)
    d3 = alloc("d3", [KC, 3, 128], BF16)
    stats3 = [alloc(f"stat{i}", [16], F32) for i in range(3)]
    lnm = alloc("lnm", [T], F32)
    lnv = alloc("lnv", [T], F32)
    lnr = alloc("lnr", [T], F32)
    lnt = alloc("lnt", [T], F32)
    ring = [alloc(f"ring{i}", [SLOT_BYTES // 2], BF16) for i in range(NSLOT)]
    act_base = cur[0]
    xres = [alloc(f"xres{i}", [NT, D], F32) for i in range(2)]
    xs = alloc("xs", [NT, D], BF16)
    junk = alloc("junk", [D], BF16)
    hb = alloc("h", [KC, T], BF16)
    tmpp = [alloc(f"tmp{i}", [T], F32) for i in range(4)]
    CVW = 640
    cvb = alloc("cvb", [KC, CVW], BF16)
    t1 = alloc("t1", [KC, T], BF16)
    UW = 640
    ub_ = alloc("u", [KC, UW], BF16)
    big_off = cur[0]
    ya = alloc("ya", [KC, T], BF16)
    cvo = alloc("cvo", [KC, T], F32)
    merged = alloc("merged", [KC, T], BF16, at=big_off)
    actb = alloc("actb", [FC, T], BF16, at=big_off)
    sq = [alloc(f"sq{i}", [T], BF16) for i in range(4)]
    zt = [alloc(f"zt{i}", [T], F32) for i in range(2)]
    ubn = alloc("ubn", [KC, T], BF16)
    h2b = alloc("h2", [KC, T], BF16, at=t1.off)
    act_end = cur[0]
    so = act_base
    NWST = 3
    wada_st = [alloc(f"wada{i}", [KC, 512], F32, at=so + i * 16384) for i in range(NWST)]
    modrow = alloc("modrow", [NMOD * D], F32, parts=1, at=so + 49152)
    brow = alloc("brow", [NMOD * D], F32, parts=1, at=so + 49152 + 24576)
    rows = alloc("rows", [3, 128], F32, at=so + 98304)
    rows2 = alloc("rows2", [128], F32, at=so + 98304 + 2048)
    dstage = [alloc(f"dstage{i}", [31, 128], BF16, at=so + 102400 + i * 8192) for i in range(2)]
    assert so + 102400 + 16384 <= act_end
    ringf = [alloc(f"ringf{i}", [SLOT_BYTES // 4], F32, at=ring[i].off) for i in range(NSLOT)]

    ps = nc.alloc_psum_tensor("ps", [128, 4096], F32)

    def psk(bank, half=None):
        if half is None:
            return [("ps", 2 * bank), ("ps", 2 * bank + 1)]
        return [("ps", 2 * bank + half)]

    def psb(bank):
        return ps[:, bank * 512:(bank + 1) * 512]

    def psb_bf(bank, half):
        return ps[:, bank * 512 + half * 256: bank * 512 + (half + 1) * 256].bitcast(BF16)

    bank_rr = [0]

    nrot = [6]

    def next_bank():
        b = bank_rr[0] % nrot[0]
        bank_rr[0] = (b + 1) % nrot[0]
        return b

    S1B, S2B = 6, 7
    tmp_rr = [0]

    def next_tmp():
        i = tmp_rr[0]
        tmp_rr[0] = (i + 1) % 4
        return tmpp[i]

    def dma(eng, out_ap, in_ap, reads, writes, **kw):
        def fn(e, out_ap=out_ap, in_ap=in_ap, kw=kw):
            return e.dma_start(out=out_ap, in_=in_ap, **kw)
        return SC.op(eng, fn, reads, writes, dma=True)

    def mm_group(out_ap, pairs, reads, writes):
        def fn(pe, out_ap=out_ap, pairs=pairs):
            n = len(pairs)
            ins = None
            for i, (l, r) in enumerate(pairs):
                ins = pe.matmul(out_ap, l, r, start=(i == 0), stop=(i == n - 1))
            return ins
        return SC.op("pe", fn, reads, writes)

    def act_op(out_ap, in_ap, func, reads, writes, scale=None, bias=None, accum=None):
        def fn(a, out_ap=out_ap, in_ap=in_ap, func=func, scale=scale, bias=bias, accum=accum):
            kw = {}
            if scale is not None:
                kw["scale"] = scale
            if bias is not None:
                kw["bias"] = bias
            if accum is not None:
                kw["accum_out"] = accum
            return a.activation(out=out_ap, in_=in_ap, func=func, **kw)
        return SC.op("act", fn, reads, writes)

    def tt(eng, out_ap, in0, in1, op, reads, writes):
        def fn(v, out_ap=out_ap, in0=in0, in1=in1, op=op):
            return v.tensor_tensor(out=out_ap, in0=in0, in1=in1, op=op)
        return SC.op(eng, fn, reads, writes)

    def ts(eng, out_ap, in0, s1, op0, reads, writes, s2=None, op1=None):
        def fn(v, out_ap=out_ap, in0=in0, s1=s1, s2=s2, op0=op0, op1=op1):
            if op1 is None:
                return v.tensor_scalar(out=out_ap, in0=in0, scalar1=s1, scalar2=None, op0=op0)
            return v.tensor_scalar(out=out_ap, in0=in0, scalar1=s1, scalar2=s2, op0=op0, op1=op1)
        return SC.op(eng, fn, reads, writes)

    def stt(out_ap, in0, scalar, in1, op0, op1, reads, writes):
        def fn(v, out_ap=out_ap, in0=in0, scalar=scalar, in1=in1, op0=op0, op1=op1):
            return v.scalar_tensor_tensor(out=out_ap, in0=in0, scalar=scalar, in1=in1, op0=op0, op1=op1)
        return SC.op("dve", fn, reads, writes)

    dma("sp", ident_f.t[:], identf_h.ap(), [], ident_f.k())
    dma("sp", ident_b.t[:], identb_h.ap(), [], ident_b.k())
    SC.op("dve", lambda v: v.memset(ones_b.t[:], 1.0), [], ones_b.k())
    SC.op("dve", lambda v: v.memset(ones_f.t[:], 1.0), [], ones_f.k())
    SC.op("dve", lambda v: v.memset(nhalf.t[:], -0.5), [], nhalf.k())
    SC.op("dve", lambda v: v.memset(epsb.t[:, 0:1], EPS), [], epsb.k())
    SC.op("dve", lambda v: v.memset(epsb.t[:, 1:2], LN_EPS), epsb.k(), epsb.k())

    SC.op("dve", lambda v: v.memset(rows.t[:], 0.0), [], rows.k())

    rowkeys = []

    def rowvec(h, off, nrow=KC, base=0):
        done = 0
        while done < nrow:
            r = off + done
            g, p0 = r // 128, r % 128
            n = min(nrow - done, 128 - p0)
            src = bass.AP(h, base + done * 128, [[128, n], [1, 128]])
            rowkeys.append(("rows", g, p0))
            dma("sp", rows.t[p0:p0 + n, g, :], src, rows.k(), [("rows", g, p0)])
            done += n

    rowvec(c_h, C_C)
    rowvec(gmix_h, C_GMIX)
    rowvec(gffn_h, C_GFFN)
    rowvec(cb_h, C_CB)
    rowvec(lng_h, C_LNG)
    rowvec(lnb_h, C_LNB)
    rowvec(w3_h, C_W3, nrow=3 * KC)
    rowvec(w31_h, C_W31, nrow=31 * KC)
    tb = next_bank()

    def _tr_rows(pe, tb=tb):
        ins = None
        for g in range(3):
            ins = pe.transpose(ps[:, tb * 512 + g * 128: tb * 512 + (g + 1) * 128], rows.t[:, g, :], ident_f.t[:])
        return ins
    SC.op("pe", _tr_rows, rows.k() + rowkeys + ident_f.k(), psk(tb))
    SC.op("dve", lambda v, tb=tb: v.tensor_copy(out=cols.t[:], in_=ps[:, tb * 512: tb * 512 + NCOL]),
          psk(tb), cols.k())

    dma("sp", GF.t[:], bass.AP(gfin_h, 0, [[0, 128], [1, D]]), [], GF.k())

    act_op(csilu.t[:], cols.t[:, C_C:C_C + KC], AF.Silu, cols.k(C_C, C_C + KC), csilu.k())
    for j in range(KC):
        for i in range(3):
            ts("dve", d3.t[:, j, i, :], ident_f.t[:], cols.t[:, C_W3 + i * KC + j:C_W3 + i * KC + j + 1], ALU.mult,
               ident_f.k() + cols.k(), d3.k((j * 3 + i) * 128, (j * 3 + i + 1) * 128))

    def npe_of(b, j):
        if b == 0:
            return 24 if j < 7 else 31
        if j < 4:
            return 0
        return (14, 14, 24, 31)[j - 4]

    def npe_max(j):
        return max(npe_of(0, j), npe_of(1, j))

    k_dg = []
    for j in range(KC):
        ds_ = dstage[j % 2]
        n = npe_max(j)
        for i in range(n):
            ts("dve", ds_.t[:, i, :], ident_f.t[:], cols.t[:, C_W31 + i * KC + j:C_W31 + i * KC + j + 1], ALU.mult,
               ident_f.k() + cols.k(), ds_.k(i * 128, (i + 1) * 128))
        key = ("dr", "dg", j)
        k_dg.append(key)
        dma("act", dg_s.ap()[j][:, 0:n * 128], ds_.t[:, 0:n, :].rearrange("p i c -> p (i c)"), ds_.k(0, n * 128), [key])

    dma("sp", brow.t[:], bass.AP(b_ada_h, 0, [[0, 1], [1, NMOD * D]]), [], brow.k())
    def wada_load(g):
        st = wada_st[g % NWST]
        src = w_ada_h.ap()[:, g * 512:(g + 1) * 512].rearrange("(k p) c -> p k c", p=128)
        dma("sp", st.t[:], src, [], st.k() + [("wada", g)])

    for g in range(NWST):
        wada_load(g)
    for g in range(NMOD * D // 512):
        st = wada_st[g % NWST]
        bank = next_bank()
        pairs = [(csilu.t[:, kc:kc + 1], st.t[:, kc, :]) for kc in range(KC)]
        mm_group(ps[0:1, bank * 512:(bank + 1) * 512], pairs, csilu.k() + st.k(), psk(bank))
        tt("dve", modrow.t[:, g * 512:(g + 1) * 512], ps[0:1, bank * 512:(bank + 1) * 512],
           brow.t[:, g * 512:(g + 1) * 512], ALU.add,
           psk(bank) + brow.k(g * 512, (g + 1) * 512), modrow.k(g * 512, (g + 1) * 512))
        if g + NWST < NMOD * D // 512:
            wada_load(g + NWST)
    kmod = [("dr", "mod")]
    dma("sp", bass.AP(mod_d, 0, [[0, 1], [1, NMOD * D]]), modrow.t[:], modrow.k(), kmod)
    SC.op("dve", lambda v: v.memset(rows2.t[:], 0.0), [], rows2.k())
    dma("sp", rows2.t[0:NMOD * KC, :], bass.AP(mod_d, 0, [[128, NMOD * KC], [1, 128]]), kmod + rows2.k(), rows2.k())
    tb2 = next_bank()
    SC.op("pe", lambda pe, tb2=tb2: pe.transpose(ps[:, tb2 * 512: tb2 * 512 + 128], rows2.t[:], ident_f.t[:]),
          rows2.k() + ident_f.k(), psk(tb2))
    SC.op("dve", lambda v, tb2=tb2: v.tensor_copy(out=modc.t[:], in_=ps[:, tb2 * 512: tb2 * 512 + NMOD * KC]),
          psk(tb2), modc.k())
    M_SH1, M_SC1, M_SH2, M_SC2 = 0, 8, 24, 32
    stt(gsh.t[:, 0:8], modc.t[:, M_SC1:M_SC1 + 8], 1.0, cols.t[:, C_GMIX:C_GMIX + 8], ALU.add, ALU.mult,
        modc.k() + cols.k(), gsh.k())
    stt(gsh.t[:, 8:16], modc.t[:, M_SC2:M_SC2 + 8], 1.0, cols.t[:, C_GFFN:C_GFFN + 8], ALU.add, ALU.mult,
        modc.k() + cols.k(), gsh.k())

    def w_item(sh, oh, c0, ncol, r0=0, kcn=KC):
        sl = lambda h: h.ap()[r0:r0 + kcn * 128, c0:c0 + ncol].rearrange("(k p) c -> p k c", p=128)
        return dict(kind="w", scr=sl(sh), src32=sl(oh), kcn=kcn, ncol=ncol, key=("dr", sh.name, c0, r0))

    def dg_item(b, j):
        n = npe_of(b, j)
        src = dg_s.ap()[j].rearrange("p (i c) -> p i c", c=128)[:, 0:n, :]
        return dict(kind="dg", scr=src, kcn=n, ncol=128, key=k_dg[j], tag=("dg", j))

    def g_item(q):
        return dict(kind="g", scr=bass.AP(mod_d, q * D, [[0, 128], [1, D]]), kcn=1, ncol=D, key=kmod[0], tag=("g", q))

    def win_item(tag, c0):
        it_ = w_item(win_s, w_in_h, c0, 512)
        it_["tag"] = tag
        return it_

    def glu_items(jj):
        return [win_item(("vc", jj), 3072 + jj * 512), win_item(("gl", jj), 4096 + jj * 512)]

    def conv_sched(b):
        sch = [[] for _ in range(KC)]
        if b == 0:
            for j in range(KC):
                n0 = npe_of(0, j)
                sch[j] = [("pe", j)] + [("tap", j, i) for i in range(n0, 31)] + [("fin", j)]
            return sch
        n4, n5, n6 = npe_of(b, 4), npe_of(b, 5), npe_of(b, 6)
        t45 = []
        for i in range(min(n4, n5), 31):
            if i >= n4:
                t45.append(("tap", 4, i))
            if i >= n5:
                t45.append(("tap", 5, i))
        q = (len(t45) + 3) // 4
        sch[1] = [("pe", 4), ("pe", 5)] + t45[0:q - 2]
        sch[2] = t45[q - 2:2 * q - 2]
        sch[3] = t45[2 * q - 2:3 * q - 2]
        sch[4] = t45[3 * q - 2:] + [("fin", 4), ("fin", 5)]
        n7 = npe_of(b, 7)
        t6 = []
        for i in range(min(n6, n7), 31):
            if i >= n6:
                t6.append(("tap", 6, i))
            if i >= n7:
                t6.append(("tap", 7, i))
        h = (len(t6) + 2) // 3
        sch[5] = [("pe", 6), ("pe", 7)] + t6[0:h]
        sch[6] = t6[h:2 * h]
        sch[7] = t6[2 * h:] + [("fin", 6), ("fin", 7)]
        return sch

    def block_items(b):
        it = []
        if b == 0:
            for jj in range(2):
                it += glu_items(jj)
        sch = conv_sched(b)
        for j in range(KC):
            if j % 4 == 0:
                jj = j // 4
                it.append(win_item(("C", jj), 1024 + jj * 512))
                it.append(win_item(("V", jj), 2048 + jj * 512))
                it.append(win_item(("B", jj), 0 + jj * 512))
            for a in sch[j]:
                if a[0] == "pe" and npe_of(b, a[1]) > 0:
                    it.append(dg_item(b, a[1]))
        for mm in range(2):
            it.append(win_item(("ga", mm), 5120 + mm * 512))
            w = w_item(wa_s, wA_h, mm * 512, 512); w["tag"] = ("wa", mm); it.append(w)
        for mm in range(2):
            it.append(win_item(("gb", mm), 6144 + mm * 512))
            w = w_item(wb_s, wB_h, mm * 512, 512); w["tag"] = ("wb", mm); it.append(w)
        if b == 0:
            it.append(g_item(2))
        for half in range(2):
            w = w_item(wo_s, wo_h, half * 512, 512); w["tag"] = ("wo", half); it.append(w)
        if b + 1 < n_blocks:
            for jj in range(2):
                it += glu_items(jj)
        for jj in range(6):
            ncol = 512 if jj < 5 else 256
            w = w_item(wfi_s, wfi_h, jj * 512, ncol); w["tag"] = ("fa", jj); it.append(w)
            w = w_item(wfi_s, wfi_h, DFF + jj * 512, ncol); w["tag"] = ("fb", jj); it.append(w)
        for half in range(2):
            if half == 0 and b == 0:
                it.append(g_item(5))
            for g, kcn in enumerate((8, 8, 6)):
                w = w_item(wfo_s, wfo_h, half * 512, 512, r0=g * 1024, kcn=kcn); w["tag"] = ("fo", half, g); it.append(w)
        return it

    items = []
    first_use = {}
    for b in range(n_blocks):
        for itm in block_items(b):
            itm = dict(itm)
            itm["first"] = itm["kind"] == "w" and itm["key"] not in first_use
            if itm["first"]:
                first_use[itm["key"]] = True
            items.append(itm)
    n_items = len(items)
    issued = [0]
    taken = [0]
    free_slots = list(range(NSLOT))
    item_slot = {}

    def slot_view(n):
        itm = items[n]
        s_ = item_slot[n]
        if itm["kind"] == "g":
            return ringf[s_], ringf[s_].t[:, 0:D], ringf[s_].k(0, D)
        kcn, ncol = itm["kcn"], itm["ncol"]
        r = ring[s_]
        return r, r.t[:, 0:kcn * ncol].rearrange("p (k c) -> p k c", c=ncol), r.k(0, kcn * ncol)

    def issue_items():
        while issued[0] < n_items and free_slots:
            n = issued[0]
            issued[0] += 1
            item_slot[n] = free_slots.pop(0)
            itm = items[n]
            r, v, keys = slot_view(n)
            if itm["first"]:
                dma("pool", v, itm["src32"], [("wada", 7)] if n < NSLOT else [], keys)
                if itm["tag"][0] not in ("wo", "fo"):
                    dma("sp", itm["scr"], v, keys, [itm["key"]])
            else:
                dma("sp", v, itm["scr"], [itm["key"]], keys)

    def take_item(tag):
        n = taken[0]
        taken[0] += 1
        assert n < issued[0], "weight ring too small for this consumption order"
        assert items[n]["tag"] == tag, (items[n]["tag"], tag)
        r, v, keys = slot_view(n)
        itm = items[n]
        if itm["first"] and itm["tag"][0] in ("wo", "fo"):
            gq = cur_gate[0]
            half = itm["tag"][1]
            for kc in range(itm["kcn"]):
                tt("dve", v[:, kc, :], v[:, kc, :], gq[2][:, half * 512:(half + 1) * 512], ALU.mult,
                   keys + gq[1].k(0, D), keys)
            dma("sp", itm["scr"], v, keys, [itm["key"]])
        return n, r, v

    cur_gate = [None]

    def release_item(n):
        free_slots.append(item_slot[n])
        issue_items()

    issue_items()

    def load_x(b):
        xr = xres[b % 2]
        src = x[b * T:(b + 1) * T, :].rearrange("(i p) d -> p i d", p=128)
        dma("sp", xr.t[:], src, [], xr.k())

    def psb_bf1k(bank):
        return ps[:, bank * 512:(bank + 1) * 512].bitcast(BF16)

    def norm_stats(xr, st, tiles):
        for i in tiles:
            act_op(junk.t[:], xr.t[:, i, :], AF.Square, xr.k(i * D, (i + 1) * D), st.k(i, i + 1),
                   accum=st.t[:, i:i + 1])
        lo, hi = tiles[0], tiles[-1] + 1
        ts("pool", st.t[:, 4 + lo:4 + hi], st.t[:, lo:hi], 1.0 / D, ALU.mult, st.k(), st.k(), s2=EPS, op1=ALU.add)
        tt("pool", st.t[:, 8 + lo:8 + hi], st.t[:, 4 + lo:4 + hi], nhalf.t[:, 0:hi - lo], ALU.pow,
           st.k() + nhalf.k(), st.k())
        for i in tiles:
            ts("dve", xs.t[:, i, :], xr.t[:, i, :], st.t[:, 8 + i:9 + i], ALU.mult,
               xr.k(i * D, (i + 1) * D) + st.k(), xs.k(i * D, (i + 1) * D))

    def norm_transpose(i, hdst, gs_off, sh_off):
        bank = next_bank()

        def fn(pe, i=i, bank=bank):
            ins = None
            o = psb_bf1k(bank)
            for k in range(KC):
                ins = pe.transpose(o[:, k * 128:(k + 1) * 128], xs.t[:, i, k * 128:(k + 1) * 128], ident_b.t[:])
            return ins
        SC.op("pe", fn, xs.k(i * D, (i + 1) * D) + ident_b.k(), psk(bank))
        for k in range(KC):
            act_op(hdst.t[:, k, i * 128:(i + 1) * 128], psb_bf1k(bank)[:, k * 128:(k + 1) * 128], AF.Identity,
                   psk(bank) + gsh.k() + modc.k(), hdst.k(k * T + i * 128, k * T + (i + 1) * 128),
                   scale=gsh.t[:, gs_off + k:gs_off + k + 1], bias=modc.t[:, sh_off + k:sh_off + k + 1])

    def proj_group(slot_r, slot_v, jl, rhs_buf):
        bank = next_bank()
        pairs = [(slot_v[:, kc, jl * 128:(jl + 1) * 128], rhs_buf.t[:, kc, :]) for kc in range(KC)]
        mm_group(psb(bank), pairs, slot_r.k() + rhs_buf.k(), psk(bank))
        return bank

    def glu_chunk(j, iVC, iGL):
        jl = j % 4
        bG = proj_group(iGL[1], iGL[2], jl, hb)
        tG = next_tmp()
        act_op(tG.t[:], psb(bG), AF.Sigmoid, psk(bG), tG.k())
        bV = proj_group(iVC[1], iVC[2], jl, hb)
        tt("dve", ub_.t[:, j, 30:30 + T], psb(bV), tG.t[:], ALU.mult,
           psk(bV) + tG.k(), ub_.k(j * UW, (j + 1) * UW))

    stat_n = [0]

    def stats(j):
        sqj = sq[j % 4]
        first, lastc = stat_n[0] % KC == 0, stat_n[0] % KC == KC - 1
        stat_n[0] += 1

        def fn(pe, j=j, sqj=sqj, first=first, lastc=lastc):
            pe.matmul(psb(S1B), ones_f.t[:], cvo.t[:, j, :], start=first, stop=lastc)
            return pe.matmul(psb(S2B), ones_b.t[:], sqj.t[:], start=first, stop=lastc)
        SC.op("pe", fn, ones_f.k() + ones_b.k() + cvo.k(j * T, (j + 1) * T) + sqj.k(), psk(S1B) + psk(S2B))

    def conv_tap(j, i, bank):
        sc = cols.t[:, C_W31 + i * KC + j:C_W31 + i * KC + j + 1]
        src = ub_.t[:, j, i:i + T]
        rk = ub_.k(j * UW, (j + 1) * UW) + cols.k() + psk(bank)
        if i == 0:
            ts("dve", psb(bank), src, sc, ALU.mult, ub_.k(j * UW, (j + 1) * UW) + cols.k(), psk(bank))
        else:
            stt(psb(bank), src, sc, psb(bank), ALU.mult, ALU.add, rk, psk(bank))

    def conv_fin(j, bank):
        ukeys = ub_.k(j * UW, (j + 1) * UW)
        act_op(cvo.t[:, j, :], psb(bank), AF.Identity, psk(bank) + cols.k(), cvo.k(j * T, (j + 1) * T),
               bias=cols.t[:, C_CB + j:C_CB + j + 1])
        sqj = sq[j % 4]
        act_op(sqj.t[:], cvo.t[:, j, :], AF.Square, cvo.k(j * T, (j + 1) * T), sqj.k())
        SC.op("dve", lambda v, j=j: v.tensor_copy(out=ub_.t[:, j, 0:30], in_=ub_.t[:, j, T:T + 30]),
              ukeys, ukeys)

    prev_final = [None]
    need_x = [False]

    def do_block(b):
        xr = xres[b % 2]
        last = b + 1 >= n_blocks
        if b == 0:
            load_x(0)
            norm_stats(xr, stats3[0], list(range(NT)))
            for i in range(NT):
                norm_transpose(i, hb, 0, M_SH1)
            for jj in range(2):
                iVC = take_item(("vc", jj))
                iGL = take_item(("gl", jj))
                for jl in range(4):
                    glu_chunk(jj * 4 + jl, iVC, iGL)
                release_item(iVC[0])
                release_item(iGL[0])
        if not last and prev_final[0] is None:
            load_x(b + 1)

        nrot[0] = 4
        stat_q = []
        if b > 0:
            for cj in range(4):
                conv_fin(cj, 4 + cj)
            stat_q.extend(range(4))
        sch = conv_sched(b)
        cbank = {}
        tB_of = {}

        def conv3(j):
            bank = next_bank()
            ck = cvb.k(j * CVW, (j + 1) * CVW)
            pairs = [(d3.t[:, j, i, :], cvb.t[:, j, i:i + T]) for i in range(3)]
            mm_group(psb(bank), pairs, d3.k() + ck, psk(bank))
            tB = tB_of.pop(j)
            tt("dve", ya.t[:, j, :], psb(bank), tB.t[:], ALU.mult, psk(bank) + tB.k(), ya.k(j * T, (j + 1) * T))
            SC.op("dve", lambda v, j=j: v.tensor_copy(out=cvb.t[:, j, 0:2], in_=cvb.t[:, j, T:T + 2]), ck, ck)

        def conv_unit(a):
            cj = a[1]
            n0 = npe_of(b, cj)
            if a[0] == "pe":
                bank = 4 + cj % 2
                cbank[cj] = bank
                if n0 > 0:
                    nD, rD, vD = take_item(("dg", cj))
                    pairs = [(vD[:, i, :], ub_.t[:, cj, i:i + T]) for i in range(n0)]
                    mm_group(psb(bank), pairs, rD.k() + ub_.k(cj * UW, (cj + 1) * UW), psk(bank))
                    release_item(nD)
            elif a[0] == "fin":
                conv_fin(cj, cbank.pop(cj))
                stat_q.append(cj)
            else:
                conv_tap(cj, a[2], cbank[cj])

        def conv_units(units, n):
            for _ in range(n):
                if units:
                    conv_unit(units.pop(0))

        for j in range(KC):
            jl = j % 4
            units = list(sch[j])
            if jl == 0:
                iC = take_item(("C", j // 4))
                iV = take_item(("V", j // 4))
                iB = take_item(("B", j // 4))
            per = (len(units) + 3) // 4
            conv_units(units, per)
            bC = proj_group(iC[1], iC[2], jl, hb)
            tC = next_tmp()
            act_op(tC.t[:], psb(bC), AF.Copy, psk(bC), tC.k())
            conv_units(units, per)
            bVv = proj_group(iV[1], iV[2], jl, hb)
            tt("dve", cvb.t[:, j, 2:2 + T], psb(bVv), tC.t[:], ALU.mult,
               psk(bVv) + tC.k(), cvb.k(j * CVW, (j + 1) * CVW))
            conv_units(units, per)
            bB = proj_group(iB[1], iB[2], jl, hb)
            tB = next_tmp()
            act_op(tB.t[:], psb(bB), AF.Copy, psk(bB), tB.k())
            tB_of[j] = tB
            if jl == 3:
                release_item(iC[0])
                release_item(iV[0])
                release_item(iB[0])
            if j >= 1:
                conv3(j - 1)
            conv_units(units, len(units))
            if j == 1 and prev_final[0] is not None:
                prev_final[0]()
                prev_final[0] = None
                need_x[0] = not last
            if j >= 1:
                for _ in range(2):
                    if stat_q:
                        stats(stat_q.pop(0))
        conv3(KC - 1)
        while len(stat_q) > 2:
            stats(stat_q.pop(0))
        nrot[0] = 6

        def ln_finalize():
            ts("dve", lnm.t[:], psb(S1B), 1.0 / D, ALU.mult, psk(S1B), lnm.k())
            tt("dve", lnt.t[:], lnm.t[:], lnm.t[:], ALU.mult, lnm.k(), lnt.k())
            stt(lnv.t[:], psb(S2B), 1.0 / D, lnt.t[:], ALU.mult, ALU.subtract, psk(S2B) + lnt.k(), lnv.k())
            act_op(lnv.t[:], lnv.t[:], AF.Sqrt, lnv.k() + epsb.k(), lnv.k(), bias=epsb.t[:, 1:2])
            SC.op("dve", lambda v: v.reciprocal(out=lnv.t[:], in_=lnv.t[:]), lnv.k(), lnv.k())
            tt("dve", lnr.t[:], lnm.t[:], lnv.t[:], ALU.mult, lnm.k() + lnv.k(), lnr.k())

        def ln_z(j):
            ck = cvo.k(j * T, (j + 1) * T)
            tt("dve", cvo.t[:, j, :], cvo.t[:, j, :], lnv.t[:], ALU.mult, ck + lnv.k(), ck)
            tt("dve", cvo.t[:, j, :], cvo.t[:, j, :], lnr.t[:], ALU.subtract, ck + lnr.k(), ck)

        def ln_silu(j):
            act_op(ubn.t[:, j, :], cvo.t[:, j, :], AF.Silu, cvo.k(j * T, (j + 1) * T) + cols.k(),
                   ubn.k(j * T, (j + 1) * T),
                   scale=cols.t[:, C_LNG + j:C_LNG + j + 1], bias=cols.t[:, C_LNB + j:C_LNB + j + 1])

        if need_x[0]:
            load_x(b + 1)
            need_x[0] = False
        for mm in range(2):
            iG = take_item(("ga", mm))
            iW = take_item(("wa", mm))
            for ml in range(4):
                m = mm * 4 + ml
                bG = proj_group(iG[1], iG[2], ml, hb)
                tG = next_tmp()
                act_op(tG.t[:], psb(bG), AF.Sigmoid, psk(bG), tG.k())
                bY = proj_group(iW[1], iW[2], ml, ya)
                tt("dve", t1.t[:, m, :], psb(bY), tG.t[:], ALU.mult, psk(bY) + tG.k(), t1.k(m * T, (m + 1) * T))
                if m == 1:
                    while stat_q:
                        stats(stat_q.pop(0))
                    ln_finalize()
                if 2 <= m <= 5:
                    ln_z(2 * (m - 2))
                    ln_z(2 * (m - 2) + 1)
                if m == 5:
                    for j in range(KC):
                        ln_silu(j)
            release_item(iG[0])
            release_item(iW[0])

        nxt = xres[(b + 1) % 2]
        for mm in range(2):
            iG = take_item(("gb", mm))
            iW = take_item(("wb", mm))
            if mm == 1 and not last:
                norm_stats(nxt, stats3[0], list(range(NT)))
            for ml in range(4):
                m = mm * 4 + ml
                bG = proj_group(iG[1], iG[2], ml, hb)
                tG = next_tmp()
                act_op(tG.t[:], psb(bG), AF.Sigmoid, psk(bG), tG.k())
                bY = proj_group(iW[1], iW[2], ml, ubn)
                z = zt[m % 2]
                tt("dve", z.t[:], psb(bY), tG.t[:], ALU.mult, psk(bY) + tG.k(), z.k())
                tt("dve", merged.t[:, m, :], z.t[:], t1.t[:, m, :], ALU.add,
                   z.k() + t1.k(m * T, (m + 1) * T), merged.k(m * T, (m + 1) * T))
            release_item(iG[0])
            release_item(iW[0])

        pre = []
        if not last:
            for i in range(31):
                for j in range(4):
                    pre.append((j, i))

        def pre_taps(n):
            for _ in range(n):
                if pre:
                    j, i = pre.pop(0)
                    conv_tap(j, i, 4 + j)

        nrot[0] = 4 if not last else 6
        iG1 = None
        if b == 0:
            iG1 = take_item(("g", 2))
            cur_gate[0] = iG1
        iW0 = take_item(("wo", 0))
        iW1 = take_item(("wo", 1))
        st2 = stats3[1]

        def gated_add(bank, iG, i, half):
            lo, hi = i * D + half * 512, i * D + (half + 1) * 512
            tt("dve", xr.t[:, i, half * 512:(half + 1) * 512], psb(bank), xr.t[:, i, half * 512:(half + 1) * 512],
               ALU.add, psk(bank) + xr.k(lo, hi), xr.k(lo, hi))

        def wo_tile(i):
            for half, iW in enumerate((iW0, iW1)):
                bank = next_bank()
                pairs = [(merged.t[:, kc, i * 128:(i + 1) * 128], iW[2][:, kc, :]) for kc in range(KC)]
                mm_group(psb(bank), pairs, merged.k() + iW[1].k(), psk(bank))
                gated_add(bank, iG1, i, half)
            norm_stats(xr, st2, [i])

        if last:
            for i in range(NT):
                wo_tile(i)
                if i >= 1:
                    norm_transpose(i - 1, h2b, 8, M_SH2)
            norm_transpose(NT - 1, h2b, 8, M_SH2)
        else:
            for i in range(NT):
                norm_transpose(i, hb, 0, M_SH1)
                wo_tile(i)
            release_item(iW0[0])
            release_item(iW1[0])
            glu_it = {}
            for i in range(NT):
                jj = i // 2
                if i % 2 == 0:
                    glu_it[jj] = (take_item(("vc", jj)), take_item(("gl", jj)))
                iVC, iGL = glu_it[jj]
                glu_chunk(2 * i, iVC, iGL)
                if i == NT - 1:
                    norm_transpose(i, h2b, 8, M_SH2)
                    glu_chunk(2 * i + 1, iVC, iGL)
                else:
                    glu_chunk(2 * i + 1, iVC, iGL)
                    norm_transpose(i, h2b, 8, M_SH2)
                if i >= 1:
                    pre_taps(8)
                if i % 2 == 1:
                    release_item(iVC[0])
                    release_item(iGL[0])
        if iG1 is not None:
            release_item(iG1[0])
        if last:
            release_item(iW0[0])
            release_item(iW1[0])

        for jj in range(6):
            iA = take_item(("fa", jj))
            iB = take_item(("fb", jj))
            for jl in range(4 if jj < 5 else 2):
                j = jj * 4 + jl
                bA = proj_group(iA[1], iA[2], jl, h2b)
                tA = next_tmp()
                act_op(tA.t[:], psb(bA), AF.Silu, psk(bA), tA.k())
                bB = proj_group(iB[1], iB[2], jl, h2b)
                tt("dve", actb.t[:, j, :], psb(bB), tA.t[:], ALU.mult, psk(bB) + tA.k(), actb.k(j * T, (j + 1) * T))
                pre_taps(4)
            release_item(iA[0])
            release_item(iB[0])

        for half in range(2):
            if half == 0:
                iG2 = None
                if b == 0:
                    iG2 = take_item(("g", 5))
                    cur_gate[0] = iG2
            its = [take_item(("fo", half, g)) for g in range(3)]
            for i in range(NT):
                bank = next_bank()
                pairs = []
                rk = []
                for g, (n_, r_, v_) in enumerate(its):
                    rk += r_.k()
                    for kk in range(8 if g < 2 else 6):
                        kc = g * 8 + kk
                        pairs.append((actb.t[:, kc, i * 128:(i + 1) * 128], v_[:, kk, :]))
                mm_group(psb(bank), pairs, actb.k() + rk, psk(bank))
                gated_add(bank, iG2, i, half)
                pre_taps(5)
            for it_ in its:
                release_item(it_[0])
        if iG2 is not None:
            release_item(iG2[0])
        pre_taps(len(pre))

        def final_norm():
            st3 = stats3[2]
            for i in range(NT):
                act_op(junk.t[:], xr.t[:, i, :], AF.Square, xr.k(i * D, (i + 1) * D), st3.k(),
                       accum=st3.t[:, i:i + 1])
            ts("pool", st3.t[:, 4:8], st3.t[:, 0:4], 1.0 / D, ALU.mult, st3.k(), st3.k(), s2=EPS, op1=ALU.add)
            tt("pool", st3.t[:, 8:12], st3.t[:, 4:8], nhalf.t[:, 0:4], ALU.pow, st3.k() + nhalf.k(), st3.k())
            for i in range(NT):
                ts("pool", xr.t[:, i, :], xr.t[:, i, :], st3.t[:, 8 + i:9 + i], ALU.mult,
                   xr.k(i * D, (i + 1) * D) + st3.k(), xr.k(i * D, (i + 1) * D), s2=1.0, op1=ALU.mult)
                tt("pool", xr.t[:, i, :], xr.t[:, i, :], GF.t[:], ALU.mult,
                   xr.k(i * D, (i + 1) * D) + GF.k(), xr.k(i * D, (i + 1) * D))
            dma("sp", out[b * T:(b + 1) * T, :].rearrange("(i p) d -> p i d", p=128), xr.t[:], xr.k(),
                [("dr", "out", b)])

        if last:
            final_norm()
        else:
            prev_final[0] = final_norm
        return None

        st3 = stats3[2]
        for i in range(NT):
            act_op(junk.t[:], xr.t[:, i, :], AF.Square, xr.k(i * D, (i + 1) * D), st3.k(),
                   accum=st3.t[:, i:i + 1])
        act_op(st3.t[:, 4:8], st3.t[:, 0:4], AF.Sqrt, st3.k() + epsb.k(), st3.k(), scale=1.0 / D, bias=epsb.t[:, 0:1])
        SC.op("dve", lambda v: v.reciprocal(out=st3.t[:, 8:12], in_=st3.t[:, 4:8]), st3.k(), st3.k())
        for i in range(NT):
            ts("pool", xr.t[:, i, :], xr.t[:, i, :], st3.t[:, 8 + i:9 + i], ALU.mult,
               xr.k(i * D, (i + 1) * D) + st3.k(), xr.k(i * D, (i + 1) * D), s2=1.0, op1=ALU.mult)
            tt("pool", xr.t[:, i, :], xr.t[:, i, :], GF.t[:], ALU.mult,
               xr.k(i * D, (i + 1) * D) + GF.k(), xr.k(i * D, (i + 1) * D))
        return dma("sp", out[b * T:(b + 1) * T, :].rearrange("(i p) d -> p i d", p=128), xr.t[:], xr.k(),
                   [("dr", "out", b)])

    SC.op("dve", lambda v: v.memset(cvb.t[:], 0.0), [], cvb.k())
    SC.op("dve", lambda v: v.memset(ub_.t[:], 0.0), [], ub_.k())
    last_out = None
    for b in range(n_blocks):
        last_out = do_block(b)

    final_reads = [("dr", "out", b) for b in range(n_blocks)]
    SC.op("sp", lambda e: e.nop(), final_reads, [])

    with nc.Block() as block:
        SC.emit(nc, block)
    return nc


_NC_CACHE = {}


def _get_nc(n_blocks):
    if n_blocks not in _NC_CACHE:
        _NC_CACHE[n_blocks] = build_nc(n_blocks)
    return _NC_CACHE[n_blocks]


def kernel(x, c, w_ada, b_ada, norm_mix_g, w_in, conv_short_w, w_short_out,
           conv_conf_w, conv_conf_b, conf_ln_g, conf_ln_b, w_conf_out, w_o,
           norm_ffn_g, w_ffn_in, w_ffn_out, final_norm_g, _n_blocks=None, _cores=None):
    f = lambda a: np.ascontiguousarray(np.asarray(a, dtype=np.float32))
    x = f(x)
    B, S, _ = x.shape
    n_blocks = S // T if _n_blocks is None else _n_blocks
    S_use = n_blocks * T
    cores = list(range(B)) if _cores is None else _cores
    shared = {
        "w_ada": f(w_ada)[0], "b_ada": f(b_ada)[0], "norm_mix_g": f(norm_mix_g)[0], "w_in": f(w_in)[0],
        "conv_short_w": f(conv_short_w)[0], "w_short_out": f(w_short_out)[0], "conv_conf_w": f(conv_conf_w)[0],
        "conv_conf_b": f(conv_conf_b)[0], "conf_ln_g": f(conf_ln_g)[0], "conf_ln_b": f(conf_ln_b)[0],
        "w_conf_out": f(w_conf_out)[0], "w_o": f(w_o)[0], "norm_ffn_g": f(norm_ffn_g)[0],
        "w_ffn_in": f(w_ffn_in)[0], "w_ffn_out": f(w_ffn_out)[0], "final_norm_g": f(final_norm_g),
        "ident_f": np.eye(128, dtype=np.float32),
        "ident_b": np.eye(128, dtype=np.float32).astype(ml_dtypes.bfloat16),
    }
    cc = f(c)
    in_maps = []
    for i in cores:
        m = dict(shared)
        m["x"] = np.ascontiguousarray(x[i, :S_use])
        m["c"] = np.ascontiguousarray(cc[i])
        in_maps.append(m)
    nc = _get_nc(n_blocks)
    res = run_bass_kernel_spmd(nc, in_maps, core_ids=list(range(len(cores))))
    outs = [np.asarray(r["out"], dtype=np.float32) for r in res.results]
    return np.stack(outs, axis=0)
```

```python
import numpy as np
import ml_dtypes
import concourse.bass as bass
import concourse.mybir as mybir
from concourse.bass_utils import run_bass_kernel_spmd

F32 = mybir.dt.float32
BF16 = mybir.dt.bfloat16
AF = mybir.ActivationFunctionType
ALU = mybir.AluOpType

D = 1024
DFF = 2816
DIN = 7168
NMOD = 6
SEQ = 8192
T = 512
NT = T // 128
KC = D // 128
FC = DFF // 128
EPS = 1e-6
LN_EPS = 1e-5
ATOM = 256
NSLOT = 7
NPE = 16
SLOT_BYTES = 8192
NLANE = 8


class _Op:
    __slots__ = ("eng", "fn", "deps", "idx", "is_dma", "signals", "dma_n", "count")


class Sched:
    ENGS = ("pe", "act", "dve", "pool", "sp")

    def __init__(self):
        self.ops = []
        self.by_eng = {e: [] for e in self.ENGS}
        self.atoms = {}
        self.dma_ops = {e: [] for e in self.ENGS}

    def op(self, eng, fn, reads=(), writes=(), dma=False):
        o = _Op()
        o.eng, o.fn, o.is_dma, o.signals, o.count, o.dma_n = eng, fn, dma, False, 0, -1
        o.idx = len(self.ops)
        deps = set()
        for k in reads:
            a = self.atoms.get(k)
            if a is not None and a[0] >= 0:
                deps.add(a[0])
        for k in writes:
            a = self.atoms.get(k)
            if a is not None:
                if a[0] >= 0:
                    deps.add(a[0])
                deps.update(a[1].values())
                deps.update(a[2])
        for k in reads:
            a = self.atoms.get(k)
            if a is None:
                a = self.atoms[k] = [-1, {}, []]
            if dma:
                a[2].append(o.idx)
            else:
                a[1][eng] = o.idx
        for k in writes:
            self.atoms[k] = [o.idx, {}, []]
        if dma:
            o.dma_n = len(self.dma_ops[eng])
            if o.dma_n >= NLANE:
                deps.add(self.dma_ops[eng][o.dma_n - NLANE].idx)
            self.dma_ops[eng].append(o)
        deps.discard(o.idx)
        if eng == "pe":
            deps = {d for d in deps if not (self.ops[d].eng == "pe" and not self.ops[d].is_dma)}
        o.deps = deps
        self.ops.append(o)
        self.by_eng[eng].append(o)
        return o

    def emit(self, nc, block):
        ops = self.ops
        for o in ops:
            for d in o.deps:
                if not ops[d].is_dma:
                    ops[d].signals = True
        for e in self.ENGS:
            c = 0
            for o in self.by_eng[e]:
                if not o.is_dma and o.signals:
                    c += 1
                o.count = c
        eng_sem = {e: nc.alloc_semaphore(name=f"sem_{e}") for e in self.ENGS}
        lane_sem = {e: [nc.alloc_semaphore(name=f"lane_{e}_{i}") for i in range(NLANE)]
                    for e in self.ENGS if self.dma_ops[e]}

        def make(e):
            def body(eng):
                known = {}
                for o in self.by_eng[e]:
                    need = {}
                    for d in o.deps:
                        dd = ops[d]
                        if dd.is_dma:
                            sem = lane_sem[dd.eng][dd.dma_n % NLANE]
                            val = 16 * (dd.dma_n // NLANE + 1)
                        else:
                            sem = eng_sem[dd.eng]
                            val = dd.count
                        key = sem.num
                        if need.get(key, (None, 0))[1] < val:
                            need[key] = (sem, val)
                    for key, (sem, val) in need.items():
                        if known.get(key, 0) < val:
                            eng.wait_ge(sem, val)
                            known[key] = val
                    ins = o.fn(eng)
                    if o.is_dma:
                        ins.then_inc(lane_sem[e][o.dma_n % NLANE], 16)
                    elif o.signals:
                        ins.then_inc(eng_sem[e], 1)
            return body

        block.tensor(make("pe"))
        block.scalar(make("act"))
        block.vector(make("dve"))
        block.gpsimd(make("pool"))
        block.sync(make("sp"))


def _sbk(lo, hi):
    return [("sb", a) for a in range(lo // ATOM, (hi - 1) // ATOM + 1)]


class Buf:
    def __init__(self, nc, name, free_shape, dtype, off, parts=128):
        self.t = nc.alloc_sbuf_tensor_at(name, [parts] + list(free_shape), dtype, offset=off)
        self.off = off
        self.es = 2 if dtype == BF16 else 4
        self.n = int(np.prod(free_shape))
        self.nbytes = self.n * self.es

    def k(self, lo=0, hi=None):
        hi = self.n if hi is None else hi
        return _sbk(self.off + lo * self.es, self.off + hi * self.es)


def build_nc(n_blocks=SEQ // T):
    nc = bass.Bass("TRN2", target_bir_lowering=False)
    S = n_blocks * T

    def din(name, shape, dt=F32):
        return nc.dram_tensor(name, list(shape), dt, kind="ExternalInput")

    x_h = din("x", [S, D])
    c_h = din("c", [D])
    w_ada_h = din("w_ada", [D, NMOD * D])
    b_ada_h = din("b_ada", [NMOD * D])
    gmix_h = din("norm_mix_g", [D])
    w_in_h = din("w_in", [D, DIN])
    w3_h = din("conv_short_w", [3, D])
    wA_h = din("w_short_out", [D, D])
    w31_h = din("conv_conf_w", [31, D])
    cb_h = din("conv_conf_b", [D])
    lng_h = din("conf_ln_g", [D])
    lnb_h = din("conf_ln_b", [D])
    wB_h = din("w_conf_out", [D, D])
    wo_h = din("w_o", [D, D])
    gffn_h = din("norm_ffn_g", [D])
    wfi_h = din("w_ffn_in", [D, 2 * DFF])
    wfo_h = din("w_ffn_out", [DFF, D])
    gfin_h = din("final_norm_g", [D])
    identf_h = din("ident_f", [128, 128])
    identb_h = din("ident_b", [128, 128], BF16)
    out_h = nc.dram_tensor("out", [S, D], F32, kind="ExternalOutput")

    def dscr(name, shape, dt=BF16):
        return nc.dram_tensor(name, list(shape), dt, kind="Internal")

    win_s = dscr("win_s", [D, DIN])
    wa_s = dscr("wa_s", [D, D])
    wb_s = dscr("wb_s", [D, D])
    wo_s = dscr("wo_s", [D, D])
    wfi_s = dscr("wfi_s", [D, 2 * DFF])
    wfo_s = dscr("wfo_s", [DFF, D])
    dg_s = dscr("dg_s", [KC, 128, 31 * 128])
    mod_d = dscr("mod_d", [NMOD * D], F32)

    x, out = x_h.ap(), out_h.ap()

    SC = Sched()

    cur = [16384 + 512]

    def alloc(name, free_shape, dtype, parts=128, at=None):
        es = 2 if dtype == BF16 else 4
        nb = int(np.prod(free_shape)) * es
        if at is None:
            off = cur[0]
            cur[0] = off + ((nb + ATOM - 1) // ATOM) * ATOM
        else:
            off = at
        assert off + nb <= 229376, (name, off, nb)
        return Buf(nc, name, free_shape, dtype, off, parts)

    ident_b = alloc("ident_b", [128], BF16)
    ident_f = alloc("ident_f", [128], F32)
    ones_b = alloc("ones_b", [128], BF16)
    ones_f = alloc("ones_f", [128], F32)
    NCOL = 384
    cols = alloc("cols", [NCOL], F32)
    C_GMIX, C_GFFN, C_CB, C_LNG, C_LNB, C_W3, C_W31, C_C = 0, 8, 16, 24, 32, 40, 64, 312
    modc = alloc("modc", [48], F32)
    gsh = alloc("gsh", [32], F32)
    csilu = alloc("csilu", [8], F32)
    epsb = alloc("epsb", [2], F32)
    nhalf = alloc("nhalf", [4], F32)
    GF = alloc("GF", [D], F32)
    d3 = alloc("d3", [KC, 3, 128], BF16)
    stats3 = [alloc(f"stat{i}", [16], F32) for i in range(3)]
    lnm = alloc("lnm", [T], F32)
    lnv = alloc("lnv", [T], F32)
    lnr = alloc("lnr", [T], F32)
    lnt = alloc("lnt", [T], F32)
    ring = [alloc(f"ring{i}", [SLOT_BYTES // 2], BF16) for i in range(NSLOT)]
    act_base = cur[0]
    xres = [alloc(f"xres{i}", [NT, D], F32) for i in range(2)]
    xs = alloc("xs", [NT, D], BF16)
    junk = alloc("junk", [D], BF16)
    hb = alloc("h", [KC, T], BF16)
    tmpp = [alloc(f"tmp{i}", [T], F32) for i in range(4)]
    CVW = 640
    cvb = alloc("cvb", [KC, CVW], BF16)
    t1 = alloc("t1", [KC, T], BF16)
    UW = 640
    ub_ = alloc("u", [KC, UW], BF16)
    big_off = cur[0]
    ya = alloc("ya", [KC, T], BF16)
    cvo = alloc("cvo", [KC, T], F32)
    merged = alloc("merged", [KC, T], BF16, at=big_off)
    actb = alloc("actb", [FC, T], BF16, at=big_off)
    sq = [alloc(f"sq{i}", [T], BF16) for i in range(4)]
    zt = [alloc(f"zt{i}", [T], F32) for i in range(2)]
    ubn = alloc("ubn", [KC, T], BF16)
    h2b = alloc("h2", [KC, T], BF16, at=t1.off)
    act_end = cur[0]
    so = act_base
    NWST = 3
    wada_st = [alloc(f"wada{i}", [KC, 512], F32, at=so + i * 16384) for i in range(NWST)]
    modrow = alloc("modrow", [NMOD * D], F32, parts=1, at=so + 49152)
    brow = alloc("brow", [NMOD * D], F32, parts=1, at=so + 49152 + 24576)
    rows = alloc("rows", [3, 128], F32, at=so + 98304)
    rows2 = alloc("rows2", [128], F32, at=so + 98304 + 2048)
    dstage = [alloc(f"dstage{i}", [31, 128], BF16, at=so + 102400 + i * 8192) for i in range(2)]
    assert so + 102400 + 16384 <= act_end
    ringf = [alloc(f"ringf{i}", [SLOT_BYTES // 4], F32, at=ring[i].off) for i in range(NSLOT)]

    ps = nc.alloc_psum_tensor("ps", [128, 4096], F32)

    def psk(bank, half=None):
        if half is None:
            return [("ps", 2 * bank), ("ps", 2 * bank + 1)]
        return [("ps", 2 * bank + half)]

    def psb(bank):
        return ps[:, bank * 512:(bank + 1) * 512]

    def psb_bf(bank, half):
        return ps[:, bank * 512 + half * 256: bank * 512 + (half + 1) * 256].bitcast(BF16)

    bank_rr = [0]

    nrot = [6]

    def next_bank():
        b = bank_rr[0] % nrot[0]
        bank_rr[0] = (b + 1) % nrot[0]
        return b

    S1B, S2B = 6, 7
    tmp_rr = [0]

    def next_tmp():
        i = tmp_rr[0]
        tmp_rr[0] = (i + 1) % 4
        return tmpp[i]

    def dma(eng, out_ap, in_ap, reads, writes, **kw):
        def fn(e, out_ap=out_ap, in_ap=in_ap, kw=kw):
            return e.dma_start(out=out_ap, in_=in_ap, **kw)
        return SC.op(eng, fn, reads, writes, dma=True)

    def mm_group(out_ap, pairs, reads, writes):
        def fn(pe, out_ap=out_ap, pairs=pairs):
            n = len(pairs)
            ins = None
            for i, (l, r) in enumerate(pairs):
                ins = pe.matmul(out_ap, l, r, start=(i == 0), stop=(i == n - 1))
            return ins
        return SC.op("pe", fn, reads, writes)

    def act_op(out_ap, in_ap, func, reads, writes, scale=None, bias=None, accum=None):
        def fn(a, out_ap=out_ap, in_ap=in_ap, func=func, scale=scale, bias=bias, accum=accum):
            kw = {}
            if scale is not None:
                kw["scale"] = scale
            if bias is not None:
                kw["bias"] = bias
            if accum is not None:
                kw["accum_out"] = accum
            return a.activation(out=out_ap, in_=in_ap, func=func, **kw)
        return SC.op("act", fn, reads, writes)

    def tt(eng, out_ap, in0, in1, op, reads, writes):
        def fn(v, out_ap=out_ap, in0=in0, in1=in1, op=op):
            return v.tensor_tensor(out=out_ap, in0=in0, in1=in1, op=op)
        return SC.op(eng, fn, reads, writes)

    def ts(eng, out_ap, in0, s1, op0, reads, writes, s2=None, op1=None):
        def fn(v, out_ap=out_ap, in0=in0, s1=s1, s2=s2, op0=op0, op1=op1):
            if op1 is None:
                return v.tensor_scalar(out=out_ap, in0=in0, scalar1=s1, scalar2=None, op0=op0)
            return v.tensor_scalar(out=out_ap, in0=in0, scalar1=s1, scalar2=s2, op0=op0, op1=op1)
        return SC.op(eng, fn, reads, writes)

    def stt(out_ap, in0, scalar, in1, op0, op1, reads, writes):
        def fn(v, out_ap=out_ap, in0=in0, scalar=scalar, in1=in1, op0=op0, op1=op1):
            return v.scalar_tensor_tensor(out=out_ap, in0=in0, scalar=scalar, in1=in1, op0=op0, op1=op1)
        return SC.op("dve", fn, reads, writes)

    dma("sp", ident_f.t[:], identf_h.ap(), [], ident_f.k())
    dma("sp", ident_b.t[:], identb_h.ap(), [], ident_b.k())
    SC.op("dve", lambda v: v.memset(ones_b.t[:], 1.0), [], ones_b.k())
    SC.op("dve", lambda v: v.memset(ones_f.t[:], 1.0), [], ones_f.k())
    SC.op("dve", lambda v: v.memset(nhalf.t[:], -0.5), [], nhalf.k())
    SC.op("dve", lambda v: v.memset(epsb.t[:, 0:1], EPS), [], epsb.k())
    SC.op("dve", lambda v: v.memset(epsb.t[:, 1:2], LN_EPS), epsb.k(), epsb.k())

    SC.op("dve", lambda v: v.memset(rows.t[:], 0.0), [], rows.k())

    rowkeys = []

    def rowvec(h, off, nrow=KC, base=0):
        done = 0
        while done < nrow:
            r = off + done
            g, p0 = r // 128, r % 128
            n = min(nrow - done, 128 - p0)
            src = bass.AP(h, base + done * 128, [[128, n], [1, 128]])
            rowkeys.append(("rows", g, p0))
            dma("sp", rows.t[p0:p0 + n, g, :], src, rows.k(), [("rows", g, p0)])
            done += n

    rowvec(c_h, C_C)
    rowvec(gmix_h, C_GMIX)
    rowvec(gffn_h, C_GFFN)
    rowvec(cb_h, C_CB)
    rowvec(lng_h, C_LNG)
    rowvec(lnb_h, C_LNB)
    rowvec(w3_h, C_W3, nrow=3 * KC)
    rowvec(w31_h, C_W31, nrow=31 * KC)
    tb = next_bank()

    def _tr_rows(pe, tb=tb):
        ins = None
        for g in range(3):
            ins = pe.transpose(ps[:, tb * 512 + g * 128: tb * 512 + (g + 1) * 128], rows.t[:, g, :], ident_f.t[:])
        return ins
    SC.op("pe", _tr_rows, rows.k() + rowkeys + ident_f.k(), psk(tb))
    SC.op("dve", lambda v, tb=tb: v.tensor_copy(out=cols.t[:], in_=ps[:, tb * 512: tb * 512 + NCOL]),
          psk(tb), cols.k())

    dma("sp", GF.t[:], bass.AP(gfin_h, 0, [[0, 128], [1, D]]), [], GF.k())

    act_op(csilu.t[:], cols.t[:, C_C:C_C + KC], AF.Silu, cols.k(C_C, C_C + KC), csilu.k())
    for j in range(KC):
        for i in range(3):
            ts("dve", d3.t[:, j, i, :], ident_f.t[:], cols.t[:, C_W3 + i * KC + j:C_W3 + i * KC + j + 1], ALU.mult,
               ident_f.k() + cols.k(), d3.k((j * 3 + i) * 128, (j * 3 + i + 1) * 128))

    def npe_of(b, j):
        if b == 0:
            return 24 if j < 7 else 31
        if j < 4:
            return 0
        return (14, 14, 24, 31)[j - 4]

    def npe_max(j):
        return max(npe_of(0, j), npe_of(1, j))

    k_dg = []
    for j in range(KC):
        ds_ = dstage[j % 2]
        n = npe_max(j)
        for i in range(n):
            ts("dve", ds_.t[:, i, :], ident_f.t[:], cols.t[:, C_W31 + i * KC + j:C_W31 + i * KC + j + 1], ALU.mult,
               ident_f.k() + cols.k(), ds_.k(i * 128, (i + 1) * 128))
        key = ("dr", "dg", j)
        k_dg.append(key)
        dma("act", dg_s.ap()[j][:, 0:n * 128], ds_.t[:, 0:n, :].rearrange("p i c -> p (i c)"), ds_.k(0, n * 128), [key])

    dma("sp", brow.t[:], bass.AP(b_ada_h, 0, [[0, 1], [1, NMOD * D]]), [], brow.k())
    def wada_load(g):
        st = wada_st[g % NWST]
        src = w_ada_h.ap()[:, g * 512:(g + 1) * 512].rearrange("(k p) c -> p k c", p=128)
        dma("sp", st.t[:], src, [], st.k() + [("wada", g)])

    for g in range(NWST):
        wada_load(g)
    for g in range(NMOD * D // 512):
        st = wada_st[g % NWST]
        bank = next_bank()
        pairs = [(csilu.t[:, kc:kc + 1], st.t[:, kc, :]) for kc in range(KC)]
        mm_group(ps[0:1, bank * 512:(bank + 1) * 512], pairs, csilu.k() + st.k(), psk(bank))
        tt("dve", modrow.t[:, g * 512:(g + 1) * 512], ps[0:1, bank * 512:(bank + 1) * 512],
           brow.t[:, g * 512:(g + 1) * 512], ALU.add,
           psk(bank) + brow.k(g * 512, (g + 1) * 512), modrow.k(g * 512, (g + 1) * 512))
        if g + NWST < NMOD * D // 512:
            wada_load(g + NWST)
    kmod = [("dr", "mod")]
    dma("sp", bass.AP(mod_d, 0, [[0, 1], [1, NMOD * D]]), modrow.t[:], modrow.k(), kmod)
    SC.op("dve", lambda v: v.memset(rows2.t[:], 0.0), [], rows2.k())
    dma("sp", rows2.t[0:NMOD * KC, :], bass.AP(mod_d, 0, [[128, NMOD * KC], [1, 128]]), kmod + rows2.k(), rows2.k())
    tb2 = next_bank()
    SC.op("pe", lambda pe, tb2=tb2: pe.transpose(ps[:, tb2 * 512: tb2 * 512 + 128], rows2.t[:], ident_f.t[:]),
          rows2.k() + ident_f.k(), psk(tb2))
    SC.op("dve", lambda v, tb2=tb2: v.tensor_copy(out=modc.t[:], in_=ps[:, tb2 * 512: tb2 * 512 + NMOD * KC]),
          psk(tb2), modc.k())
    M_SH1, M_SC1, M_SH2, M_SC2 = 0, 8, 24, 32
    stt(gsh.t[:, 0:8], modc.t[:, M_SC1:M_SC1 + 8], 1.0, cols.t[:, C_GMIX:C_GMIX + 8], ALU.add, ALU.mult,
        modc.k() + cols.k(), gsh.k())
    stt(gsh.t[:, 8:16], modc.t[:, M_SC2:M_SC2 + 8], 1.0, cols.t[:, C_GFFN:C_GFFN + 8], ALU.add, ALU.mult,
        modc.k() + cols.k(), gsh.k())

    def w_item(sh, oh, c0, ncol, r0=0, kcn=KC):
        sl = lambda h: h.ap()[r0:r0 + kcn * 128, c0:c0 + ncol].rearrange("(k p) c -> p k c", p=128)
        return dict(kind="w", scr=sl(sh), src32=sl(oh), kcn=kcn, ncol=ncol, key=("dr", sh.name, c0, r0))

    def dg_item(b, j):
        n = npe_of(b, j)
        src = dg_s.ap()[j].rearrange("p (i c) -> p i c", c=128)[:, 0:n, :]
        return dict(kind="dg", scr=src, kcn=n, ncol=128, key=k_dg[j], tag=("dg", j))

    def g_item(q):
        return dict(kind="g", scr=bass.AP(mod_d, q * D, [[0, 128], [1, D]]), kcn=1, ncol=D, key=kmod[0], tag=("g", q))

    def win_item(tag, c0):
        it_ = w_item(win_s, w_in_h, c0, 512)
        it_["tag"] = tag
        return it_

    def glu_items(jj):
        return [win_item(("vc", jj), 3072 + jj * 512), win_item(("gl", jj), 4096 + jj * 512)]

    def conv_sched(b):
        sch = [[] for _ in range(KC)]
        if b == 0:
            for j in range(KC):
                n0 = npe_of(0, j)
                sch[j] = [("pe", j)] + [("tap", j, i) for i in range(n0, 31)] + [("fin", j)]
            return sch
        n4, n5, n6 = npe_of(b, 4), npe_of(b, 5), npe_of(b, 6)
        t45 = []
        for i in range(min(n4, n5), 31):
            if i >= n4:
                t45.append(("tap", 4, i))
            if i >= n5:
                t45.append(("tap", 5, i))
        q = (len(t45) + 3) // 4
        sch[1] = [("pe", 4), ("pe", 5)] + t45[0:q - 2]
        sch[2] = t45[q - 2:2 * q - 2]
        sch[3] = t45[2 * q - 2:3 * q - 2]
        sch[4] = t45[3 * q - 2:] + [("fin", 4), ("fin", 5)]
        n7 = npe_of(b, 7)
        t6 = []
        for i in range(min(n6, n7), 31):
            if i >= n6:
                t6.append(("tap", 6, i))
            if i >= n7:
                t6.append(("tap", 7, i))
        h = (len(t6) + 2) // 3
        sch[5] = [("pe", 6), ("pe", 7)] + t6[0:h]
        sch[6] = t6[h:2 * h]
        sch[7] = t6[2 * h:] + [("fin", 6), ("fin", 7)]
        return sch

    def block_items(b):
        it = []
        if b == 0:
            for jj in range(2):
                it += glu_items(jj)
        sch = conv_sched(b)
        for j in range(KC):
            if j % 4 == 0:
                jj = j // 4
                it.append(win_item(("C", jj), 1024 + jj * 512))
                it.append(win_item(("V", jj), 2048 + jj * 512))
                it.append(win_item(("B", jj), 0 + jj * 512))
            for a in sch[j]:
                if a[0] == "pe" and npe_of(b, a[1]) > 0:
                    it.append(dg_item(b, a[1]))
        for mm in range(2):
            it.append(win_item(("ga", mm), 5120 + mm * 512))
            w = w_item(wa_s, wA_h, mm * 512, 512); w["tag"] = ("wa", mm); it.append(w)
        for mm in range(2):
            it.append(win_item(("gb", mm), 6144 + mm * 512))
            w = w_item(wb_s, wB_h, mm * 512, 512); w["tag"] = ("wb", mm); it.append(w)
        if b == 0:
            it.append(g_item(2))
        for half in range(2):
            w = w_item(wo_s, wo_h, half * 512, 512); w["tag"] = ("wo", half); it.append(w)
        if b + 1 < n_blocks:
            for jj in range(2):
                it += glu_items(jj)
        for jj in range(6):
            ncol = 512 if jj < 5 else 256
            w = w_item(wfi_s, wfi_h, jj * 512, ncol); w["tag"] = ("fa", jj); it.append(w)
            w = w_item(wfi_s, wfi_h, DFF + jj * 512, ncol); w["tag"] = ("fb", jj); it.append(w)
        for half in range(2):
            if half == 0 and b == 0:
                it.append(g_item(5))
            for g, kcn in enumerate((8, 8, 6)):
                w = w_item(wfo_s, wfo_h, half * 512, 512, r0=g * 1024, kcn=kcn); w["tag"] = ("fo", half, g); it.append(w)
        return it

    items = []
    first_use = {}
    for b in range(n_blocks):
        for itm in block_items(b):
            itm = dict(itm)
            itm["first"] = itm["kind"] == "w" and itm["key"] not in first_use
            if itm["first"]:
                first_use[itm["key"]] = True
            items.append(itm)
    n_items = len(items)
    issued = [0]
    taken = [0]
    free_slots = list(range(NSLOT))
    item_slot = {}

    def slot_view(n):
        itm = items[n]
        s_ = item_slot[n]
        if itm["kind"] == "g":
            return ringf[s_], ringf[s_].t[:, 0:D], ringf[s_].k(0, D)
        kcn, ncol = itm["kcn"], itm["ncol"]
        r = ring[s_]
        return r, r.t[:, 0:kcn * ncol].rearrange("p (k c) -> p k c", c=ncol), r.k(0, kcn * ncol)

    def issue_items():
        while issued[0] < n_items and free_slots:
            n = issued[0]
            issued[0] += 1
            item_slot[n] = free_slots.pop(0)
            itm = items[n]
            r, v, keys = slot_view(n)
            if itm["first"]:
                dma("pool", v, itm["src32"], [("wada", 7)] if n < NSLOT else [], keys)
                if itm["tag"][0] not in ("wo", "fo"):
                    dma("sp", itm["scr"], v, keys, [itm["key"]])
            else:
                dma("sp", v, itm["scr"], [itm["key"]], keys)

    def take_item(tag):
        n = taken[0]
        taken[0] += 1
        assert n < issued[0], "weight ring too small for this consumption order"
        assert items[n]["tag"] == tag, (items[n]["tag"], tag)
        r, v, keys = slot_view(n)
        itm = items[n]
        if itm["first"] and itm["tag"][0] in ("wo", "fo"):
            gq = cur_gate[0]
            half = itm["tag"][1]
            for kc in range(itm["kcn"]):
                tt("dve", v[:, kc, :], v[:, kc, :], gq[2][:, half * 512:(half + 1) * 512], ALU.mult,
                   keys + gq[1].k(0, D), keys)
            dma("sp", itm["scr"], v, keys, [itm["key"]])
        return n, r, v

    cur_gate = [None]

    def release_item(n):
        free_slots.append(item_slot[n])
        issue_items()

    issue_items()

    def load_x(b):
        xr = xres[b % 2]
        src = x[b * T:(b + 1) * T, :].rearrange("(i p) d -> p i d", p=128)
        dma("sp", xr.t[:], src, [], xr.k())

    def psb_bf1k(bank):
        return ps[:, bank * 512:(bank + 1) * 512].bitcast(BF16)

    def norm_stats(xr, st, tiles):
        for i in tiles:
            act_op(junk.t[:], xr.t[:, i, :], AF.Square, xr.k(i * D, (i + 1) * D), st.k(i, i + 1),
                   accum=st.t[:, i:i + 1])
        lo, hi = tiles[0], tiles[-1] + 1
        ts("pool", st.t[:, 4 + lo:4 + hi], st.t[:, lo:hi], 1.0 / D, ALU.mult, st.k(), st.k(), s2=EPS, op1=ALU.add)
        tt("pool", st.t[:, 8 + lo:8 + hi], st.t[:, 4 + lo:4 + hi], nhalf.t[:, 0:hi - lo], ALU.pow,
           st.k() + nhalf.k(), st.k())
        for i in tiles:
            ts("dve", xs.t[:, i, :], xr.t[:, i, :], st.t[:, 8 + i:9 + i], ALU.mult,
               xr.k(i * D, (i + 1) * D) + st.k(), xs.k(i * D, (i + 1) * D))

    def norm_transpose(i, hdst, gs_off, sh_off):
        bank = next_bank()

        def fn(pe, i=i, bank=bank):
            ins = None
            o = psb_bf1k(bank)
            for k in range(KC):
                ins = pe.transpose(o[:, k * 128:(k + 1) * 128], xs.t[:, i, k * 128:(k + 1) * 128], ident_b.t[:])
            return ins
        SC.op("pe", fn, xs.k(i * D, (i + 1) * D) + ident_b.k(), psk(bank))
        for k in range(KC):
            act_op(hdst.t[:, k, i * 128:(i + 1) * 128], psb_bf1k(bank)[:, k * 128:(k + 1) * 128], AF.Identity,
                   psk(bank) + gsh.k() + modc.k(), hdst.k(k * T + i * 128, k * T + (i + 1) * 128),
                   scale=gsh.t[:, gs_off + k:gs_off + k + 1], bias=modc.t[:, sh_off + k:sh_off + k + 1])

    def proj_group(slot_r, slot_v, jl, rhs_buf):
        bank = next_bank()
        pairs = [(slot_v[:, kc, jl * 128:(jl + 1) * 128], rhs_buf.t[:, kc, :]) for kc in range(KC)]
        mm_group(psb(bank), pairs, slot_r.k() + rhs_buf.k(), psk(bank))
        return bank

    def glu_chunk(j, iVC, iGL):
        jl = j % 4
        bG = proj_group(iGL[1], iGL[2], jl, hb)
        tG = next_tmp()
        act_op(tG.t[:], psb(bG), AF.Sigmoid, psk(bG), tG.k())
        bV = proj_group(iVC[1], iVC[2], jl, hb)
        tt("dve", ub_.t[:, j, 30:30 + T], psb(bV), tG.t[:], ALU.mult,
           psk(bV) + tG.k(), ub_.k(j * UW, (j + 1) * UW))

    stat_n = [0]

    def stats(j):
        sqj = sq[j % 4]
        first, lastc = stat_n[0] % KC == 0, stat_n[0] % KC == KC - 1
        stat_n[0] += 1

        def fn(pe, j=j, sqj=sqj, first=first, lastc=lastc):
            pe.matmul(psb(S1B), ones_f.t[:], cvo.t[:, j, :], start=first, stop=lastc)
            return pe.matmul(psb(S2B), ones_b.t[:], sqj.t[:], start=first, stop=lastc)
        SC.op("pe", fn, ones_f.k() + ones_b.k() + cvo.k(j * T, (j + 1) * T) + sqj.k(), psk(S1B) + psk(S2B))

    def conv_tap(j, i, bank):
        sc = cols.t[:, C_W31 + i * KC + j:C_W31 + i * KC + j + 1]
        src = ub_.t[:, j, i:i + T]
        rk = ub_.k(j * UW, (j + 1) * UW) + cols.k() + psk(bank)
        if i == 0:
            ts("dve", psb(bank), src, sc, ALU.mult, ub_.k(j * UW, (j + 1) * UW) + cols.k(), psk(bank))
        else:
            stt(psb(bank), src, sc, psb(bank), ALU.mult, ALU.add, rk, psk(bank))

    def conv_fin(j, bank):
        ukeys = ub_.k(j * UW, (j + 1) * UW)
        act_op(cvo.t[:, j, :], psb(bank), AF.Identity, psk(bank) + cols.k(), cvo.k(j * T, (j + 1) * T),
               bias=cols.t[:, C_CB + j:C_CB + j + 1])
        sqj = sq[j % 4]
        act_op(sqj.t[:], cvo.t[:, j, :], AF.Square, cvo.k(j * T, (j + 1) * T), sqj.k())
        SC.op("dve", lambda v, j=j: v.tensor_copy(out=ub_.t[:, j, 0:30], in_=ub_.t[:, j, T:T + 30]),
              ukeys, ukeys)

    prev_final = [None]
    need_x = [False]

    def do_block(b):
        xr = xres[b % 2]
        last = b + 1 >= n_blocks
        if b == 0:
            load_x(0)
            norm_stats(xr, stats3[0], list(range(NT)))
            for i in range(NT):
                norm_transpose(i, hb, 0, M_SH1)
            for jj in range(2):
                iVC = take_item(("vc", jj))
                iGL = take_item(("gl", jj))
                for jl in range(4):
                    glu_chunk(jj * 4 + jl, iVC, iGL)
                release_item(iVC[0])
                release_item(iGL[0])
        if not last and prev_final[0] is None:
            load_x(b + 1)

        nrot[0] = 4
        stat_q = []
        if b > 0:
            for cj in range(4):
                conv_fin(cj, 4 + cj)
            stat_q.extend(range(4))
        sch = conv_sched(b)
        cbank = {}
        tB_of = {}

        def conv3(j):
            bank = next_bank()
            ck = cvb.k(j * CVW, (j + 1) * CVW)
            pairs = [(d3.t[:, j, i, :], cvb.t[:, j, i:i + T]) for i in range(3)]
            mm_group(psb(bank), pairs, d3.k() + ck, psk(bank))
            tB = tB_of.pop(j)
            tt("dve", ya.t[:, j, :], psb(bank), tB.t[:], ALU.mult, psk(bank) + tB.k(), ya.k(j * T, (j + 1) * T))
            SC.op("dve", lambda v, j=j: v.tensor_copy(out=cvb.t[:, j, 0:2], in_=cvb.t[:, j, T:T + 2]), ck, ck)

        def conv_unit(a):
            cj = a[1]
            n0 = npe_of(b, cj)
            if a[0] == "pe":
                bank = 4 + cj % 2
                cbank[cj] = bank
                if n0 > 0:
                    nD, rD, vD = take_item(("dg", cj))
                    pairs = [(vD[:, i, :], ub_.t[:, cj, i:i + T]) for i in range(n0)]
                    mm_group(psb(bank), pairs, rD.k() + ub_.k(cj * UW, (cj + 1) * UW), psk(bank))
                    release_item(nD)
            elif a[0] == "fin":
                conv_fin(cj, cbank.pop(cj))
                stat_q.append(cj)
            else:
                conv_tap(cj, a[2], cbank[cj])

        def conv_units(units, n):
            for _ in range(n):
                if units:
                    conv_unit(units.pop(0))

        for j in range(KC):
            jl = j % 4
            units = list(sch[j])
            if jl == 0:
                iC = take_item(("C", j // 4))
                iV = take_item(("V", j // 4))
                iB = take_item(("B", j // 4))
            per = (len(units) + 3) // 4
            conv_units(units, per)
            bC = proj_group(iC[1], iC[2], jl, hb)
            tC = next_tmp()
            act_op(tC.t[:], psb(bC), AF.Copy, psk(bC), tC.k())
            conv_units(units, per)
            bVv = proj_group(iV[1], iV[2], jl, hb)
            tt("dve", cvb.t[:, j, 2:2 + T], psb(bVv), tC.t[:], ALU.mult,
               psk(bVv) + tC.k(), cvb.k(j * CVW, (j + 1) * CVW))
            conv_units(units, per)
            bB = proj_group(iB[1], iB[2], jl, hb)
            tB = next_tmp()
            act_op(tB.t[:], psb(bB), AF.Copy, psk(bB), tB.k())
            tB_of[j] = tB
            if jl == 3:
                release_item(iC[0])
                release_item(iV[0])
                release_item(iB[0])
            if j >= 1:
                conv3(j - 1)
            conv_units(units, len(units))
            if j == 1 and prev_final[0] is not None:
                prev_final[0]()
                prev_final[0] = None
                need_x[0] = not last
            if j >= 1:
                for _ in range(2):
                    if stat_q:
                        stats(stat_q.pop(0))
        conv3(KC - 1)
        while len(stat_q) > 2:
            stats(stat_q.pop(0))
        nrot[0] = 6

        def ln_finalize():
            ts("dve", lnm.t[:], psb(S1B), 1.0 / D, ALU.mult, psk(S1B), lnm.k())
            tt("dve", lnt.t[:], lnm.t[:], lnm.t[:], ALU.mult, lnm.k(), lnt.k())
            stt(lnv.t[:], psb(S2B), 1.0 / D, lnt.t[:], ALU.mult, ALU.subtract, psk(S2B) + lnt.k(), lnv.k())
            act_op(lnv.t[:], lnv.t[:], AF.Sqrt, lnv.k() + epsb.k(), lnv.k(), bias=epsb.t[:, 1:2])
            SC.op("dve", lambda v: v.reciprocal(out=lnv.t[:], in_=lnv.t[:]), lnv.k(), lnv.k())
            tt("dve", lnr.t[:], lnm.t[:], lnv.t[:], ALU.mult, lnm.k() + lnv.k(), lnr.k())

        def ln_z(j):
            ck = cvo.k(j * T, (j + 1) * T)
            tt("dve", cvo.t[:, j, :], cvo.t[:, j, :], lnv.t[:], ALU.mult, ck + lnv.k(), ck)
            tt("dve", cvo.t[:, j, :], cvo.t[:, j, :], lnr.t[:], ALU.subtract, ck + lnr.k(), ck)

        def ln_silu(j):
            act_op(ubn.t[:, j, :], cvo.t[:, j, :], AF.Silu, cvo.k(j * T, (j + 1) * T) + cols.k(),
                   ubn.k(j * T, (j + 1) * T),
                   scale=cols.t[:, C_LNG + j:C_LNG + j + 1], bias=cols.t[:, C_LNB + j:C_LNB + j + 1])

        if need_x[0]:
            load_x(b + 1)
            need_x[0] = False
        for mm in range(2):
            iG = take_item(("ga", mm))
            iW = take_item(("wa", mm))
            for ml in range(4):
                m = mm * 4 + ml
                bG = proj_group(iG[1], iG[2], ml, hb)
                tG = next_tmp()
                act_op(tG.t[:], psb(bG), AF.Sigmoid, psk(bG), tG.k())
                bY = proj_group(iW[1], iW[2], ml, ya)
                tt("dve", t1.t[:, m, :], psb(bY), tG.t[:], ALU.mult, psk(bY) + tG.k(), t1.k(m * T, (m + 1) * T))
                if m == 1:
                    while stat_q:
                        stats(stat_q.pop(0))
                    ln_finalize()
                if 2 <= m <= 5:
                    ln_z(2 * (m - 2))
                    ln_z(2 * (m - 2) + 1)
                if m == 5:
                    for j in range(KC):
                        ln_silu(j)
            release_item(iG[0])
            release_item(iW[0])

        nxt = xres[(b + 1) % 2]
        for mm in range(2):
            iG = take_item(("gb", mm))
            iW = take_item(("wb", mm))
            if mm == 1 and not last:
                norm_stats(nxt, stats3[0], list(range(NT)))
            for ml in range(4):
                m = mm * 4 + ml
                bG = proj_group(iG[1], iG[2], ml, hb)
                tG = next_tmp()
                act_op(tG.t[:], psb(bG), AF.Sigmoid, psk(bG), tG.k())
                bY = proj_group(iW[1], iW[2], ml, ubn)
                z = zt[m % 2]
                tt("dve", z.t[:], psb(bY), tG.t[:], ALU.mult, psk(bY) + tG.k(), z.k())
                tt("dve", merged.t[:, m, :], z.t[:], t1.t[:, m, :], ALU.add,
                   z.k() + t1.k(m * T, (m + 1) * T), merged.k(m * T, (m + 1) * T))
            release_item(iG[0])
            release_item(iW[0])

        pre = []
        if not last:
            for i in range(31):
                for j in range(4):
                    pre.append((j, i))

        def pre_taps(n):
            for _ in range(n):
                if pre:
                    j, i = pre.pop(0)
                    conv_tap(j, i, 4 + j)

        nrot[0] = 4 if not last else 6
        iG1 = None
        if b == 0:
            iG1 = take_item(("g", 2))
            cur_gate[0] = iG1
        iW0 = take_item(("wo", 0))
        iW1 = take_item(("wo", 1))
        st2 = stats3[1]

        def gated_add(bank, iG, i, half):
            lo, hi = i * D + half * 512, i * D + (half + 1) * 512
            tt("dve", xr.t[:, i, half * 512:(half + 1) * 512], psb(bank), xr.t[:, i, half * 512:(half + 1) * 512],
               ALU.add, psk(bank) + xr.k(lo, hi), xr.k(lo, hi))

        def wo_tile(i):
            for half, iW in enumerate((iW0, iW1)):
                bank = next_bank()
                pairs = [(merged.t[:, kc, i * 128:(i + 1) * 128], iW[2][:, kc, :]) for kc in range(KC)]
                mm_group(psb(bank), pairs, merged.k() + iW[1].k(), psk(bank))
                gated_add(bank, iG1, i, half)
            norm_stats(xr, st2, [i])

        if last:
            for i in range(NT):
                wo_tile(i)
                if i >= 1:
                    norm_transpose(i - 1, h2b, 8, M_SH2)
            norm_transpose(NT - 1, h2b, 8, M_SH2)
        else:
            for i in range(NT):
                norm_transpose(i, hb, 0, M_SH1)
                wo_tile(i)
            release_item(iW0[0])
            release_item(iW1[0])
            glu_it = {}
            for i in range(NT):
                jj = i // 2
                if i % 2 == 0:
                    glu_it[jj] = (take_item(("vc", jj)), take_item(("gl", jj)))
                iVC, iGL = glu_it[jj]
                glu_chunk(2 * i, iVC, iGL)
                if i == NT - 1:
                    norm_transpose(i, h2b, 8, M_SH2)
                    glu_chunk(2 * i + 1, iVC, iGL)
                else:
                    glu_chunk(2 * i + 1, iVC, iGL)
                    norm_transpose(i, h2b, 8, M_SH2)
                if i >= 1:
                    pre_taps(8)
                if i % 2 == 1:
                    release_item(iVC[0])
                    release_item(iGL[0])
        if iG1 is not None:
            release_item(iG1[0])
        if last:
            release_item(iW0[0])
            release_item(iW1[0])

        for jj in range(6):
            iA = take_item(("fa", jj))
            iB = take_item(("fb", jj))
            for jl in range(4 if jj < 5 else 2):
                j = jj * 4 + jl
                bA = proj_group(iA[1], iA[2], jl, h2b)
                tA = next_tmp()
                act_op(tA.t[:], psb(bA), AF.Silu, psk(bA), tA.k())
                bB = proj_group(iB[1], iB[2], jl, h2b)
                tt("dve", actb.t[:, j, :], psb(bB), tA.t[:], ALU.mult, psk(bB) + tA.k(), actb.k(j * T, (j + 1) * T))
                pre_taps(4)
            release_item(iA[0])
            release_item(iB[0])

        for half in range(2):
            if half == 0:
                iG2 = None
                if b == 0:
                    iG2 = take_item(("g", 5))
                    cur_gate[0] = iG2
            its = [take_item(("fo", half, g)) for g in range(3)]
            for i in range(NT):
                bank = next_bank()
                pairs = []
                rk = []
                for g, (n_, r_, v_) in enumerate(its):
                    rk += r_.k()
                    for kk in range(8 if g < 2 else 6):
                        kc = g * 8 + kk
                        pairs.append((actb.t[:, kc, i * 128:(i + 1) * 128], v_[:, kk, :]))
                mm_group(psb(bank), pairs, actb.k() + rk, psk(bank))
                gated_add(bank, iG2, i, half)
                pre_taps(6)
            for it_ in its:
                release_item(it_[0])
        if iG2 is not None:
            release_item(iG2[0])
        pre_taps(len(pre))

        def final_norm():
            st3 = stats3[2]
            for i in range(NT):
                act_op(junk.t[:], xr.t[:, i, :], AF.Square, xr.k(i * D, (i + 1) * D), st3.k(),
                       accum=st3.t[:, i:i + 1])
            ts("pool", st3.t[:, 4:8], st3.t[:, 0:4], 1.0 / D, ALU.mult, st3.k(), st3.k(), s2=EPS, op1=ALU.add)
            tt("pool", st3.t[:, 8:12], st3.t[:, 4:8], nhalf.t[:, 0:4], ALU.pow, st3.k() + nhalf.k(), st3.k())
            for i in range(NT):
                ts("pool", xr.t[:, i, :], xr.t[:, i, :], st3.t[:, 8 + i:9 + i], ALU.mult,
                   xr.k(i * D, (i + 1) * D) + st3.k(), xr.k(i * D, (i + 1) * D), s2=1.0, op1=ALU.mult)
                tt("pool", xr.t[:, i, :], xr.t[:, i, :], GF.t[:], ALU.mult,
                   xr.k(i * D, (i + 1) * D) + GF.k(), xr.k(i * D, (i + 1) * D))
            dma("sp", out[b * T:(b + 1) * T, :].rearrange("(i p) d -> p i d", p=128), xr.t[:], xr.k(),
                [("dr", "out", b)])

        if last:
            final_norm()
        else:
            prev_final[0] = final_norm
        return None

        st3 = stats3[2]
        for i in range(NT):
            act_op(junk.t[:], xr.t[:, i, :], AF.Square, xr.k(i * D, (i + 1) * D), st3.k(),
                   accum=st3.t[:, i:i + 1])
        act_op(st3.t[:, 4:8], st3.t[:, 0:4], AF.Sqrt, st3.k() + epsb.k(), st3.k(), scale=1.0 / D, bias=epsb.t[:, 0:1])
        SC.op("dve", lambda v: v.reciprocal(out=st3.t[:, 8:12], in_=st3.t[:, 4:8]), st3.k(), st3.k())
        for i in range(NT):
            ts("pool", xr.t[:, i, :], xr.t[:, i, :], st3.t[:, 8 + i:9 + i], ALU.mult,
               xr.k(i * D, (i + 1) * D) + st3.k(), xr.k(i * D, (i + 1) * D), s2=1.0, op1=ALU.mult)
            tt("pool", xr.t[:, i, :], xr.t[:, i, :], GF.t[:], ALU.mult,
               xr.k(i * D, (i + 1) * D) + GF.k(), xr.k(i * D, (i + 1) * D))
        return dma("sp", out[b * T:(b + 1) * T, :].rearrange("(i p) d -> p i d", p=128), xr.t[:], xr.k(),
                   [("dr", "out", b)])

    SC.op("dve", lambda v: v.memset(cvb.t[:], 0.0), [], cvb.k())
    SC.op("dve", lambda v: v.memset(ub_.t[:], 0.0), [], ub_.k())
    last_out = None
    for b in range(n_blocks):
        last_out = do_block(b)

    final_reads = [("dr", "out", b) for b in range(n_blocks)]
    SC.op("sp", lambda e: e.nop(), final_reads, [])

    with nc.Block() as block:
        SC.emit(nc, block)
    return nc


_NC_CACHE = {}


def _get_nc(n_blocks):
    if n_blocks not in _NC_CACHE:
        _NC_CACHE[n_blocks] = build_nc(n_blocks)
    return _NC_CACHE[n_blocks]


def kernel(x, c, w_ada, b_ada, norm_mix_g, w_in, conv_short_w, w_short_out,
           conv_conf_w, conv_conf_b, conf_ln_g, conf_ln_b, w_conf_out, w_o,
           norm_ffn_g, w_ffn_in, w_ffn_out, final_norm_g, _n_blocks=None, _cores=None):
    f = lambda a: np.ascontiguousarray(np.asarray(a, dtype=np.float32))
    x = f(x)
    B, S, _ = x.shape
    n_blocks = S // T if _n_blocks is None else _n_blocks
    S_use = n_blocks * T
    cores = list(range(B)) if _cores is None else _cores
    shared = {
        "w_ada": f(w_ada)[0], "b_ada": f(b_ada)[0], "norm_mix_g": f(norm_mix_g)[0], "w_in": f(w_in)[0],
        "conv_short_w": f(conv_short_w)[0], "w_short_out": f(w_short_out)[0], "conv_conf_w": f(conv_conf_w)[0],
        "conv_conf_b": f(conv_conf_b)[0], "conf_ln_g": f(conf_ln_g)[0], "conf_ln_b": f(conf_ln_b)[0],
        "w_conf_out": f(w_conf_out)[0], "w_o": f(w_o)[0], "norm_ffn_g": f(norm_ffn_g)[0],
        "w_ffn_in": f(w_ffn_in)[0], "w_ffn_out": f(w_ffn_out)[0], "final_norm_g": f(final_norm_g),
        "ident_f": np.eye(128, dtype=np.float32),
        "ident_b": np.eye(128, dtype=np.float32).astype(ml_dtypes.bfloat16),
    }
    cc = f(c)
    in_maps = []
    for i in cores:
        m = dict(shared)
        m["x"] = np.ascontiguousarray(x[i, :S_use])
        m["c"] = np.ascontiguousarray(cc[i])
        in_maps.append(m)
    nc = _get_nc(n_blocks)
    res = run_bass_kernel_spmd(nc, in_maps, core_ids=list(range(len(cores))))
    outs = [np.asarray(r["out"], dtype=np.float32) for r in res.results]
    return np.stack(outs, axis=0)
```

```python
import numpy as np
import ml_dtypes
import concourse.bass as bass
import concourse.mybir as mybir
from concourse.bass_utils import run_bass_kernel_spmd

F32 = mybir.dt.float32
BF16 = mybir.dt.bfloat16
AF = mybir.ActivationFunctionType
ALU = mybir.AluOpType

D = 1024
DFF = 2816
DIN = 7168
NMOD = 6
SEQ = 8192
T = 512
NT = T // 128
KC = D // 128
FC = DFF // 128
EPS = 1e-6
LN_EPS = 1e-5
ATOM = 256
NSLOT = 7
NPE = 16
SLOT_BYTES = 8192
NLANE = 16


class _Op:
    __slots__ = ("eng", "fn", "deps", "idx", "is_dma", "signals", "dma_n", "count")


class Sched:
    ENGS = ("pe", "act", "dve", "pool", "sp")

    def __init__(self):
        self.ops = []
        self.by_eng = {e: [] for e in self.ENGS}
        self.atoms = {}
        self.dma_ops = {e: [] for e in self.ENGS}

    def op(self, eng, fn, reads=(), writes=(), dma=False):
        o = _Op()
        o.eng, o.fn, o.is_dma, o.signals, o.count, o.dma_n = eng, fn, dma, False, 0, -1
        o.idx = len(self.ops)
        deps = set()
        for k in reads:
            a = self.atoms.get(k)
            if a is not None and a[0] >= 0:
                deps.add(a[0])
        for k in writes:
            a = self.atoms.get(k)
            if a is not None:
                if a[0] >= 0:
                    deps.add(a[0])
                deps.update(a[1].values())
                deps.update(a[2])
        for k in reads:
            a = self.atoms.get(k)
            if a is None:
                a = self.atoms[k] = [-1, {}, []]
            if dma:
                a[2].append(o.idx)
            else:
                a[1][eng] = o.idx
        for k in writes:
            self.atoms[k] = [o.idx, {}, []]
        if dma:
            o.dma_n = len(self.dma_ops[eng])
            if o.dma_n >= NLANE:
                deps.add(self.dma_ops[eng][o.dma_n - NLANE].idx)
            self.dma_ops[eng].append(o)
        deps.discard(o.idx)
        if eng == "pe":
            deps = {d for d in deps if not (self.ops[d].eng == "pe" and not self.ops[d].is_dma)}
        o.deps = deps
        self.ops.append(o)
        self.by_eng[eng].append(o)
        return o

    def emit(self, nc, block):
        ops = self.ops
        for o in ops:
            for d in o.deps:
                if not ops[d].is_dma:
                    ops[d].signals = True
        for e in self.ENGS:
            c = 0
            for o in self.by_eng[e]:
                if not o.is_dma and o.signals:
                    c += 1
                o.count = c
        eng_sem = {e: nc.alloc_semaphore(name=f"sem_{e}") for e in self.ENGS}
        lane_sem = {e: [nc.alloc_semaphore(name=f"lane_{e}_{i}") for i in range(NLANE)]
                    for e in self.ENGS if self.dma_ops[e]}

        def make(e):
            def body(eng):
                known = {}
                for o in self.by_eng[e]:
                    need = {}
                    for d in o.deps:
                        dd = ops[d]
                        if dd.is_dma:
                            sem = lane_sem[dd.eng][dd.dma_n % NLANE]
                            val = 16 * (dd.dma_n // NLANE + 1)
                        else:
                            sem = eng_sem[dd.eng]
                            val = dd.count
                        key = sem.num
                        if need.get(key, (None, 0))[1] < val:
                            need[key] = (sem, val)
                    for key, (sem, val) in need.items():
                        if known.get(key, 0) < val:
                            eng.wait_ge(sem, val)
                            known[key] = val
                    ins = o.fn(eng)
                    if o.is_dma:
                        ins.then_inc(lane_sem[e][o.dma_n % NLANE], 16)
                    elif o.signals:
                        ins.then_inc(eng_sem[e], 1)
            return body

        block.tensor(make("pe"))
        block.scalar(make("act"))
        block.vector(make("dve"))
        block.gpsimd(make("pool"))
        block.sync(make("sp"))


def _sbk(lo, hi):
    return [("sb", a) for a in range(lo // ATOM, (hi - 1) // ATOM + 1)]


class Buf:
    def __init__(self, nc, name, free_shape, dtype, off, parts=128):
        self.t = nc.alloc_sbuf_tensor_at(name, [parts] + list(free_shape), dtype, offset=off)
        self.off = off
        self.es = 2 if dtype == BF16 else 4
        self.n = int(np.prod(free_shape))
        self.nbytes = self.n * self.es

    def k(self, lo=0, hi=None):
        hi = self.n if hi is None else hi
        return _sbk(self.off + lo * self.es, self.off + hi * self.es)


def build_nc(n_blocks=SEQ // T):
    nc = bass.Bass("TRN2", target_bir_lowering=False)
    S = n_blocks * T

    def din(name, shape, dt=F32):
        return nc.dram_tensor(name, list(shape), dt, kind="ExternalInput")

    x_h = din("x", [S, D])
    c_h = din("c", [D])
    w_ada_h = din("w_ada", [D, NMOD * D])
    b_ada_h = din("b_ada", [NMOD * D])
    gmix_h = din("norm_mix_g", [D])
    w_in_h = din("w_in", [D, DIN])
    w3_h = din("conv_short_w", [3, D])
    wA_h = din("w_short_out", [D, D])
    w31_h = din("conv_conf_w", [31, D])
    cb_h = din("conv_conf_b", [D])
    lng_h = din("conf_ln_g", [D])
    lnb_h = din("conf_ln_b", [D])
    wB_h = din("w_conf_out", [D, D])
    wo_h = din("w_o", [D, D])
    gffn_h = din("norm_ffn_g", [D])
    wfi_h = din("w_ffn_in", [D, 2 * DFF])
    wfo_h = din("w_ffn_out", [DFF, D])
    gfin_h = din("final_norm_g", [D])
    identf_h = din("ident_f", [128, 128])
    identb_h = din("ident_b", [128, 128], BF16)
    out_h = nc.dram_tensor("out", [S, D], F32, kind="ExternalOutput")

    def dscr(name, shape, dt=BF16):
        return nc.dram_tensor(name, list(shape), dt, kind="Internal")

    win_s = dscr("win_s", [D, DIN])
    wa_s = dscr("wa_s", [D, D])
    wb_s = dscr("wb_s", [D, D])
    wo_s = dscr("wo_s", [D, D])
    wfi_s = dscr("wfi_s", [D, 2 * DFF])
    wfo_s = dscr("wfo_s", [DFF, D])
    dg_s = dscr("dg_s", [KC, 128, 31 * 128])
    mod_d = dscr("mod_d", [NMOD * D], F32)

    x, out = x_h.ap(), out_h.ap()

    SC = Sched()

    cur = [16384 + 512]

    def alloc(name, free_shape, dtype, parts=128, at=None):
        es = 2 if dtype == BF16 else 4
        nb = int(np.prod(free_shape)) * es
        if at is None:
            off = cur[0]
            cur[0] = off + ((nb + ATOM - 1) // ATOM) * ATOM
        else:
            off = at
        assert off + nb <= 229376, (name, off, nb)
        return Buf(nc, name, free_shape, dtype, off, parts)

    ident_b = alloc("ident_b", [128], BF16)
    ident_f = alloc("ident_f", [128], F32)
    ones_b = alloc("ones_b", [128], BF16)
    ones_f = alloc("ones_f", [128], F32)
    NCOL = 384
    cols = alloc("cols", [NCOL], F32)
    C_GMIX, C_GFFN, C_CB, C_LNG, C_LNB, C_W3, C_W31, C_C = 0, 8, 16, 24, 32, 40, 64, 312
    modc = alloc("modc", [48], F32)
    gsh = alloc("gsh", [32], F32)
    csilu = alloc("csilu", [8], F32)
    epsb = alloc("epsb", [2], F32)
    nhalf = alloc("nhalf", [4], F32)
    GF = alloc("GF", [D], F32)
    d3 = alloc("d3", [KC, 3, 128], BF16)
    stats3 = [alloc(f"stat{i}", [16], F32) for i in range(3)]
    lnm = alloc("lnm", [T], F32)
    lnv = alloc("lnv", [T], F32)
    lnr = alloc("lnr", [T], F32)
    lnt = alloc("lnt", [T], F32)
    ring = [alloc(f"ring{i}", [SLOT_BYTES // 2], BF16) for i in range(NSLOT)]
    act_base = cur[0]
    xres = [alloc(f"xres{i}", [NT, D], F32) for i in range(2)]
    xs = alloc("xs", [NT, D], BF16)
    junk = alloc("junk", [D], BF16)
    hb = alloc("h", [KC, T], BF16)
    tmpp = [alloc(f"tmp{i}", [T], F32) for i in range(4)]
    CVW = 640
    cvb = alloc("cvb", [KC, CVW], BF16)
    t1 = alloc("t1", [KC, T], BF16)
    UW = 640
    ub_ = alloc("u", [KC, UW], BF16)
    big_off = cur[0]
    ya = alloc("ya", [KC, T], BF16)
    cvo = alloc("cvo", [KC, T], F32)
    merged = alloc("merged", [KC, T], BF16, at=big_off)
    actb = alloc("actb", [FC, T], BF16, at=big_off)
    sq = [alloc(f"sq{i}", [T], BF16) for i in range(4)]
    zt = [alloc(f"zt{i}", [T], F32) for i in range(2)]
    ubn = alloc("ubn", [KC, T], BF16)
    h2b = alloc("h2", [KC, T], BF16, at=t1.off)
    act_end = cur[0]
    so = act_base
    NWST = 3
    wada_st = [alloc(f"wada{i}", [KC, 512], F32, at=so + i * 16384) for i in range(NWST)]
    modrow = alloc("modrow", [NMOD * D], F32, parts=1, at=so + 49152)
    brow = alloc("brow", [NMOD * D], F32, parts=1, at=so + 49152 + 24576)
    rows = alloc("rows", [3, 128], F32, at=so + 98304)
    rows2 = alloc("rows2", [128], F32, at=so + 98304 + 2048)
    dstage = [alloc(f"dstage{i}", [31, 128], BF16, at=so + 102400 + i * 8192) for i in range(2)]
    assert so + 102400 + 16384 <= act_end
    ringf = [alloc(f"ringf{i}", [SLOT_BYTES // 4], F32, at=ring[i].off) for i in range(NSLOT)]

    ps = nc.alloc_psum_tensor("ps", [128, 4096], F32)

    def psk(bank, half=None):
        if half is None:
            return [("ps", 2 * bank), ("ps", 2 * bank + 1)]
        return [("ps", 2 * bank + half)]

    def psb(bank):
        return ps[:, bank * 512:(bank + 1) * 512]

    def psb_bf(bank, half):
        return ps[:, bank * 512 + half * 256: bank * 512 + (half + 1) * 256].bitcast(BF16)

    bank_rr = [0]

    nrot = [6]

    def next_bank():
        b = bank_rr[0] % nrot[0]
        bank_rr[0] = (b + 1) % nrot[0]
        return b

    S1B, S2B = 6, 7
    tmp_rr = [0]

    def next_tmp():
        i = tmp_rr[0]
        tmp_rr[0] = (i + 1) % 4
        return tmpp[i]

    def dma(eng, out_ap, in_ap, reads, writes, **kw):
        def fn(e, out_ap=out_ap, in_ap=in_ap, kw=kw):
            return e.dma_start(out=out_ap, in_=in_ap, **kw)
        return SC.op(eng, fn, reads, writes, dma=True)

    def mm_group(out_ap, pairs, reads, writes):
        def fn(pe, out_ap=out_ap, pairs=pairs):
            n = len(pairs)
            ins = None
            for i, (l, r) in enumerate(pairs):
                ins = pe.matmul(out_ap, l, r, start=(i == 0), stop=(i == n - 1))
            return ins
        return SC.op("pe", fn, reads, writes)

    def act_op(out_ap, in_ap, func, reads, writes, scale=None, bias=None, accum=None):
        def fn(a, out_ap=out_ap, in_ap=in_ap, func=func, scale=scale, bias=bias, accum=accum):
            kw = {}
            if scale is not None:
                kw["scale"] = scale
            if bias is not None:
                kw["bias"] = bias
            if accum is not None:
                kw["accum_out"] = accum
            return a.activation(out=out_ap, in_=in_ap, func=func, **kw)
        return SC.op("act", fn, reads, writes)

    def tt(eng, out_ap, in0, in1, op, reads, writes):
        def fn(v, out_ap=out_ap, in0=in0, in1=in1, op=op):
            return v.tensor_tensor(out=out_ap, in0=in0, in1=in1, op=op)
        return SC.op(eng, fn, reads, writes)

    def ts(eng, out_ap, in0, s1, op0, reads, writes, s2=None, op1=None):
        def fn(v, out_ap=out_ap, in0=in0, s1=s1, s2=s2, op0=op0, op1=op1):
            if op1 is None:
                return v.tensor_scalar(out=out_ap, in0=in0, scalar1=s1, scalar2=None, op0=op0)
            return v.tensor_scalar(out=out_ap, in0=in0, scalar1=s1, scalar2=s2, op0=op0, op1=op1)
        return SC.op(eng, fn, reads, writes)

    def stt(out_ap, in0, scalar, in1, op0, op1, reads, writes):
        def fn(v, out_ap=out_ap, in0=in0, scalar=scalar, in1=in1, op0=op0, op1=op1):
            return v.scalar_tensor_tensor(out=out_ap, in0=in0, scalar=scalar, in1=in1, op0=op0, op1=op1)
        return SC.op("dve", fn, reads, writes)

    dma("sp", ident_f.t[:], identf_h.ap(), [], ident_f.k())
    dma("sp", ident_b.t[:], identb_h.ap(), [], ident_b.k())
    SC.op("dve", lambda v: v.memset(ones_b.t[:], 1.0), [], ones_b.k())
    SC.op("dve", lambda v: v.memset(ones_f.t[:], 1.0), [], ones_f.k())
    SC.op("dve", lambda v: v.memset(nhalf.t[:], -0.5), [], nhalf.k())
    SC.op("dve", lambda v: v.memset(epsb.t[:, 0:1], EPS), [], epsb.k())
    SC.op("dve", lambda v: v.memset(epsb.t[:, 1:2], LN_EPS), epsb.k(), epsb.k())

    SC.op("dve", lambda v: v.memset(rows.t[:], 0.0), [], rows.k())

    rowkeys = []

    def rowvec(h, off, nrow=KC, base=0):
        done = 0
        while done < nrow:
            r = off + done
            g, p0 = r // 128, r % 128
            n = min(nrow - done, 128 - p0)
            src = bass.AP(h, base + done * 128, [[128, n], [1, 128]])
            rowkeys.append(("rows", g, p0))
            dma("sp", rows.t[p0:p0 + n, g, :], src, rows.k(), [("rows", g, p0)])
            done += n

    rowvec(c_h, C_C)
    rowvec(gmix_h, C_GMIX)
    rowvec(gffn_h, C_GFFN)
    rowvec(cb_h, C_CB)
    rowvec(lng_h, C_LNG)
    rowvec(lnb_h, C_LNB)
    rowvec(w3_h, C_W3, nrow=3 * KC)
    rowvec(w31_h, C_W31, nrow=31 * KC)
    tb = next_bank()

    def _tr_rows(pe, tb=tb):
        ins = None
        for g in range(3):
            ins = pe.transpose(ps[:, tb * 512 + g * 128: tb * 512 + (g + 1) * 128], rows.t[:, g, :], ident_f.t[:])
        return ins
    SC.op("pe", _tr_rows, rows.k() + rowkeys + ident_f.k(), psk(tb))
    SC.op("dve", lambda v, tb=tb: v.tensor_copy(out=cols.t[:], in_=ps[:, tb * 512: tb * 512 + NCOL]),
          psk(tb), cols.k())

    dma("sp", GF.t[:], bass.AP(gfin_h, 0, [[0, 128], [1, D]]), [], GF.k())

    act_op(csilu.t[:], cols.t[:, C_C:C_C + KC], AF.Silu, cols.k(C_C, C_C + KC), csilu.k())
    for j in range(KC):
        for i in range(3):
            ts("dve", d3.t[:, j, i, :], ident_f.t[:], cols.t[:, C_W3 + i * KC + j:C_W3 + i * KC + j + 1], ALU.mult,
               ident_f.k() + cols.k(), d3.k((j * 3 + i) * 128, (j * 3 + i + 1) * 128))

    def npe_of(b, j):
        if b == 0:
            return 24 if j < 7 else 31
        if j < 4:
            return 0
        return (14, 14, 24, 31)[j - 4]

    def npe_max(j):
        return max(npe_of(0, j), npe_of(1, j))

    k_dg = []
    for j in range(KC):
        ds_ = dstage[j % 2]
        n = npe_max(j)
        for i in range(n):
            ts("dve", ds_.t[:, i, :], ident_f.t[:], cols.t[:, C_W31 + i * KC + j:C_W31 + i * KC + j + 1], ALU.mult,
               ident_f.k() + cols.k(), ds_.k(i * 128, (i + 1) * 128))
        key = ("dr", "dg", j)
        k_dg.append(key)
        dma("act", dg_s.ap()[j][:, 0:n * 128], ds_.t[:, 0:n, :].rearrange("p i c -> p (i c)"), ds_.k(0, n * 128), [key])

    dma("sp", brow.t[:], bass.AP(b_ada_h, 0, [[0, 1], [1, NMOD * D]]), [], brow.k())
    def wada_load(g):
        st = wada_st[g % NWST]
        src = w_ada_h.ap()[:, g * 512:(g + 1) * 512].rearrange("(k p) c -> p k c", p=128)
        dma("sp", st.t[:], src, [], st.k() + [("wada", g)])

    for g in range(NWST):
        wada_load(g)
    for g in range(NMOD * D // 512):
        st = wada_st[g % NWST]
        bank = next_bank()
        pairs = [(csilu.t[:, kc:kc + 1], st.t[:, kc, :]) for kc in range(KC)]
        mm_group(ps[0:1, bank * 512:(bank + 1) * 512], pairs, csilu.k() + st.k(), psk(bank))
        tt("dve", modrow.t[:, g * 512:(g + 1) * 512], ps[0:1, bank * 512:(bank + 1) * 512],
           brow.t[:, g * 512:(g + 1) * 512], ALU.add,
           psk(bank) + brow.k(g * 512, (g + 1) * 512), modrow.k(g * 512, (g + 1) * 512))
        if g + NWST < NMOD * D // 512:
            wada_load(g + NWST)
    kmod = [("dr", "mod")]
    dma("sp", bass.AP(mod_d, 0, [[0, 1], [1, NMOD * D]]), modrow.t[:], modrow.k(), kmod)
    SC.op("dve", lambda v: v.memset(rows2.t[:], 0.0), [], rows2.k())
    dma("sp", rows2.t[0:NMOD * KC, :], bass.AP(mod_d, 0, [[128, NMOD * KC], [1, 128]]), kmod + rows2.k(), rows2.k())
    tb2 = next_bank()
    SC.op("pe", lambda pe, tb2=tb2: pe.transpose(ps[:, tb2 * 512: tb2 * 512 + 128], rows2.t[:], ident_f.t[:]),
          rows2.k() + ident_f.k(), psk(tb2))
    SC.op("dve", lambda v, tb2=tb2: v.tensor_copy(out=modc.t[:], in_=ps[:, tb2 * 512: tb2 * 512 + NMOD * KC]),
          psk(tb2), modc.k())
    M_SH1, M_SC1, M_SH2, M_SC2 = 0, 8, 24, 32
    stt(gsh.t[:, 0:8], modc.t[:, M_SC1:M_SC1 + 8], 1.0, cols.t[:, C_GMIX:C_GMIX + 8], ALU.add, ALU.mult,
        modc.k() + cols.k(), gsh.k())
    stt(gsh.t[:, 8:16], modc.t[:, M_SC2:M_SC2 + 8], 1.0, cols.t[:, C_GFFN:C_GFFN + 8], ALU.add, ALU.mult,
        modc.k() + cols.k(), gsh.k())

    def w_item(sh, oh, c0, ncol, r0=0, kcn=KC):
        sl = lambda h: h.ap()[r0:r0 + kcn * 128, c0:c0 + ncol].rearrange("(k p) c -> p k c", p=128)
        return dict(kind="w", scr=sl(sh), src32=sl(oh), kcn=kcn, ncol=ncol, key=("dr", sh.name, c0, r0))

    def dg_item(b, j):
        n = npe_of(b, j)
        src = dg_s.ap()[j].rearrange("p (i c) -> p i c", c=128)[:, 0:n, :]
        return dict(kind="dg", scr=src, kcn=n, ncol=128, key=k_dg[j], tag=("dg", j))

    def g_item(q):
        return dict(kind="g", scr=bass.AP(mod_d, q * D, [[0, 128], [1, D]]), kcn=1, ncol=D, key=kmod[0], tag=("g", q))

    def win_item(tag, c0):
        it_ = w_item(win_s, w_in_h, c0, 512)
        it_["tag"] = tag
        return it_

    def glu_items(jj):
        return [win_item(("vc", jj), 3072 + jj * 512), win_item(("gl", jj), 4096 + jj * 512)]

    def conv_sched(b):
        sch = [[] for _ in range(KC)]
        if b == 0:
            for j in range(KC):
                n0 = npe_of(0, j)
                sch[j] = [("pe", j)] + [("tap", j, i) for i in range(n0, 31)] + [("fin", j)]
            return sch
        n4, n5, n6 = npe_of(b, 4), npe_of(b, 5), npe_of(b, 6)
        t45 = []
        for i in range(min(n4, n5), 31):
            if i >= n4:
                t45.append(("tap", 4, i))
            if i >= n5:
                t45.append(("tap", 5, i))
        q = (len(t45) + 3) // 4
        sch[1] = [("pe", 4), ("pe", 5)] + t45[0:q - 2]
        sch[2] = t45[q - 2:2 * q - 2]
        sch[3] = t45[2 * q - 2:3 * q - 2]
        sch[4] = t45[3 * q - 2:] + [("fin", 4), ("fin", 5)]
        n7 = npe_of(b, 7)
        t6 = []
        for i in range(min(n6, n7), 31):
            if i >= n6:
                t6.append(("tap", 6, i))
            if i >= n7:
                t6.append(("tap", 7, i))
        h = (len(t6) + 2) // 3
        sch[5] = [("pe", 6), ("pe", 7)] + t6[0:h]
        sch[6] = t6[h:2 * h]
        sch[7] = t6[2 * h:] + [("fin", 6), ("fin", 7)]
        return sch

    def block_items(b):
        it = []
        if b == 0:
            for jj in range(2):
                it += glu_items(jj)
        sch = conv_sched(b)
        for j in range(KC):
            if j % 4 == 0:
                jj = j // 4
                it.append(win_item(("C", jj), 1024 + jj * 512))
                it.append(win_item(("V", jj), 2048 + jj * 512))
                it.append(win_item(("B", jj), 0 + jj * 512))
            for a in sch[j]:
                if a[0] == "pe" and npe_of(b, a[1]) > 0:
                    it.append(dg_item(b, a[1]))
        for mm in range(2):
            it.append(win_item(("ga", mm), 5120 + mm * 512))
            w = w_item(wa_s, wA_h, mm * 512, 512); w["tag"] = ("wa", mm); it.append(w)
        for mm in range(2):
            it.append(win_item(("gb", mm), 6144 + mm * 512))
            w = w_item(wb_s, wB_h, mm * 512, 512); w["tag"] = ("wb", mm); it.append(w)
        if b == 0:
            it.append(g_item(2))
        for half in range(2):
            w = w_item(wo_s, wo_h, half * 512, 512); w["tag"] = ("wo", half); it.append(w)
        if b + 1 < n_blocks:
            for jj in range(2):
                it += glu_items(jj)
        for jj in range(6):
            ncol = 512 if jj < 5 else 256
            w = w_item(wfi_s, wfi_h, jj * 512, ncol); w["tag"] = ("fa", jj); it.append(w)
            w = w_item(wfi_s, wfi_h, DFF + jj * 512, ncol); w["tag"] = ("fb", jj); it.append(w)
        for half in range(2):
            if half == 0 and b == 0:
                it.append(g_item(5))
            for g, kcn in enumerate((8, 8, 6)):
                w = w_item(wfo_s, wfo_h, half * 512, 512, r0=g * 1024, kcn=kcn); w["tag"] = ("fo", half, g); it.append(w)
        return it

    items = []
    first_use = {}
    for b in range(n_blocks):
        for itm in block_items(b):
            itm = dict(itm)
            itm["first"] = itm["kind"] == "w" and itm["key"] not in first_use
            if itm["first"]:
                first_use[itm["key"]] = True
            items.append(itm)
    n_items = len(items)
    issued = [0]
    taken = [0]
    free_slots = list(range(NSLOT))
    item_slot = {}

    def slot_view(n):
        itm = items[n]
        s_ = item_slot[n]
        if itm["kind"] == "g":
            return ringf[s_], ringf[s_].t[:, 0:D], ringf[s_].k(0, D)
        kcn, ncol = itm["kcn"], itm["ncol"]
        r = ring[s_]
        return r, r.t[:, 0:kcn * ncol].rearrange("p (k c) -> p k c", c=ncol), r.k(0, kcn * ncol)

    def issue_items():
        while issued[0] < n_items and free_slots:
            n = issued[0]
            issued[0] += 1
            item_slot[n] = free_slots.pop(0)
            itm = items[n]
            r, v, keys = slot_view(n)
            if itm["first"]:
                dma("pool", v, itm["src32"], [("wada", 7)] if n < NSLOT else [], keys)
                if itm["tag"][0] not in ("wo", "fo"):
                    dma("sp", itm["scr"], v, keys, [itm["key"]])
            else:
                dma("sp", v, itm["scr"], [itm["key"]], keys)

    def take_item(tag):
        n = taken[0]
        taken[0] += 1
        assert n < issued[0], "weight ring too small for this consumption order"
        assert items[n]["tag"] == tag, (items[n]["tag"], tag)
        r, v, keys = slot_view(n)
        itm = items[n]
        if itm["first"] and itm["tag"][0] in ("wo", "fo"):
            gq = cur_gate[0]
            half = itm["tag"][1]
            for kc in range(itm["kcn"]):
                tt("dve", v[:, kc, :], v[:, kc, :], gq[2][:, half * 512:(half + 1) * 512], ALU.mult,
                   keys + gq[1].k(0, D), keys)
            dma("sp", itm["scr"], v, keys, [itm["key"]])
        return n, r, v

    cur_gate = [None]

    def release_item(n):
        free_slots.append(item_slot[n])
        issue_items()

    issue_items()

    def load_x(b):
        xr = xres[b % 2]
        src = x[b * T:(b + 1) * T, :].rearrange("(i p) d -> p i d", p=128)
        dma("sp", xr.t[:], src, [], xr.k())

    def psb_bf1k(bank):
        return ps[:, bank * 512:(bank + 1) * 512].bitcast(BF16)

    def norm_stats(xr, st, tiles):
        for i in tiles:
            act_op(junk.t[:], xr.t[:, i, :], AF.Square, xr.k(i * D, (i + 1) * D), st.k(i, i + 1),
                   accum=st.t[:, i:i + 1])
        lo, hi = tiles[0], tiles[-1] + 1
        ts("pool", st.t[:, 4 + lo:4 + hi], st.t[:, lo:hi], 1.0 / D, ALU.mult, st.k(), st.k(), s2=EPS, op1=ALU.add)
        tt("pool", st.t[:, 8 + lo:8 + hi], st.t[:, 4 + lo:4 + hi], nhalf.t[:, 0:hi - lo], ALU.pow,
           st.k() + nhalf.k(), st.k())
        for i in tiles:
            ts("dve", xs.t[:, i, :], xr.t[:, i, :], st.t[:, 8 + i:9 + i], ALU.mult,
               xr.k(i * D, (i + 1) * D) + st.k(), xs.k(i * D, (i + 1) * D))

    def norm_transpose(i, hdst, gs_off, sh_off):
        bank = next_bank()

        def fn(pe, i=i, bank=bank):
            ins = None
            o = psb_bf1k(bank)
            for k in range(KC):
                ins = pe.transpose(o[:, k * 128:(k + 1) * 128], xs.t[:, i, k * 128:(k + 1) * 128], ident_b.t[:])
            return ins
        SC.op("pe", fn, xs.k(i * D, (i + 1) * D) + ident_b.k(), psk(bank))
        for k in range(KC):
            act_op(hdst.t[:, k, i * 128:(i + 1) * 128], psb_bf1k(bank)[:, k * 128:(k + 1) * 128], AF.Identity,
                   psk(bank) + gsh.k() + modc.k(), hdst.k(k * T + i * 128, k * T + (i + 1) * 128),
                   scale=gsh.t[:, gs_off + k:gs_off + k + 1], bias=modc.t[:, sh_off + k:sh_off + k + 1])

    def proj_group(slot_r, slot_v, jl, rhs_buf):
        bank = next_bank()
        pairs = [(slot_v[:, kc, jl * 128:(jl + 1) * 128], rhs_buf.t[:, kc, :]) for kc in range(KC)]
        mm_group(psb(bank), pairs, slot_r.k() + rhs_buf.k(), psk(bank))
        return bank

    def glu_chunk(j, iVC, iGL):
        jl = j % 4
        bG = proj_group(iGL[1], iGL[2], jl, hb)
        tG = next_tmp()
        act_op(tG.t[:], psb(bG), AF.Sigmoid, psk(bG), tG.k())
        bV = proj_group(iVC[1], iVC[2], jl, hb)
        tt("dve", ub_.t[:, j, 30:30 + T], psb(bV), tG.t[:], ALU.mult,
           psk(bV) + tG.k(), ub_.k(j * UW, (j + 1) * UW))

    stat_n = [0]

    def stats(j):
        sqj = sq[j % 4]
        first, lastc = stat_n[0] % KC == 0, stat_n[0] % KC == KC - 1
        stat_n[0] += 1

        def fn(pe, j=j, sqj=sqj, first=first, lastc=lastc):
            pe.matmul(psb(S1B), ones_f.t[:], cvo.t[:, j, :], start=first, stop=lastc)
            return pe.matmul(psb(S2B), ones_b.t[:], sqj.t[:], start=first, stop=lastc)
        SC.op("pe", fn, ones_f.k() + ones_b.k() + cvo.k(j * T, (j + 1) * T) + sqj.k(), psk(S1B) + psk(S2B))

    def conv_tap(j, i, bank):
        sc = cols.t[:, C_W31 + i * KC + j:C_W31 + i * KC + j + 1]
        src = ub_.t[:, j, i:i + T]
        rk = ub_.k(j * UW, (j + 1) * UW) + cols.k() + psk(bank)
        if i == 0:
            ts("dve", psb(bank), src, sc, ALU.mult, ub_.k(j * UW, (j + 1) * UW) + cols.k(), psk(bank))
        else:
            stt(psb(bank), src, sc, psb(bank), ALU.mult, ALU.add, rk, psk(bank))

    def conv_fin(j, bank):
        ukeys = ub_.k(j * UW, (j + 1) * UW)
        act_op(cvo.t[:, j, :], psb(bank), AF.Identity, psk(bank) + cols.k(), cvo.k(j * T, (j + 1) * T),
               bias=cols.t[:, C_CB + j:C_CB + j + 1])
        sqj = sq[j % 4]
        act_op(sqj.t[:], cvo.t[:, j, :], AF.Square, cvo.k(j * T, (j + 1) * T), sqj.k())
        SC.op("dve", lambda v, j=j: v.tensor_copy(out=ub_.t[:, j, 0:30], in_=ub_.t[:, j, T:T + 30]),
              ukeys, ukeys)

    prev_final = [None]
    need_x = [False]

    def do_block(b):
        xr = xres[b % 2]
        last = b + 1 >= n_blocks
        if b == 0:
            load_x(0)
            norm_stats(xr, stats3[0], list(range(NT)))
            for i in range(NT):
                norm_transpose(i, hb, 0, M_SH1)
            for jj in range(2):
                iVC = take_item(("vc", jj))
                iGL = take_item(("gl", jj))
                for jl in range(4):
                    glu_chunk(jj * 4 + jl, iVC, iGL)
                release_item(iVC[0])
                release_item(iGL[0])
        if not last and prev_final[0] is None:
            load_x(b + 1)

        nrot[0] = 4
        stat_q = []
        if b > 0:
            for cj in range(4):
                conv_fin(cj, 4 + cj)
            stat_q.extend(range(4))
        sch = conv_sched(b)
        cbank = {}
        tB_of = {}

        def conv3(j):
            bank = next_bank()
            ck = cvb.k(j * CVW, (j + 1) * CVW)
            pairs = [(d3.t[:, j, i, :], cvb.t[:, j, i:i + T]) for i in range(3)]
            mm_group(psb(bank), pairs, d3.k() + ck, psk(bank))
            tB = tB_of.pop(j)
            tt("dve", ya.t[:, j, :], psb(bank), tB.t[:], ALU.mult, psk(bank) + tB.k(), ya.k(j * T, (j + 1) * T))
            SC.op("dve", lambda v, j=j: v.tensor_copy(out=cvb.t[:, j, 0:2], in_=cvb.t[:, j, T:T + 2]), ck, ck)

        def conv_unit(a):
            cj = a[1]
            n0 = npe_of(b, cj)
            if a[0] == "pe":
                bank = 4 + cj % 2
                cbank[cj] = bank
                if n0 > 0:
                    nD, rD, vD = take_item(("dg", cj))
                    pairs = [(vD[:, i, :], ub_.t[:, cj, i:i + T]) for i in range(n0)]
                    mm_group(psb(bank), pairs, rD.k() + ub_.k(cj * UW, (cj + 1) * UW), psk(bank))
                    release_item(nD)
            elif a[0] == "fin":
                conv_fin(cj, cbank.pop(cj))
                stat_q.append(cj)
            else:
                conv_tap(cj, a[2], cbank[cj])

        def conv_units(units, n):
            for _ in range(n):
                if units:
                    conv_unit(units.pop(0))

        for j in range(KC):
            jl = j % 4
            units = list(sch[j])
            if jl == 0:
                iC = take_item(("C", j // 4))
                iV = take_item(("V", j // 4))
                iB = take_item(("B", j // 4))
            per = (len(units) + 3) // 4
            conv_units(units, per)
            bC = proj_group(iC[1], iC[2], jl, hb)
            tC = next_tmp()
            act_op(tC.t[:], psb(bC), AF.Copy, psk(bC), tC.k())
            conv_units(units, per)
            bVv = proj_group(iV[1], iV[2], jl, hb)
            tt("dve", cvb.t[:, j, 2:2 + T], psb(bVv), tC.t[:], ALU.mult,
               psk(bVv) + tC.k(), cvb.k(j * CVW, (j + 1) * CVW))
            conv_units(units, per)
            bB = proj_group(iB[1], iB[2], jl, hb)
            tB = next_tmp()
            act_op(tB.t[:], psb(bB), AF.Copy, psk(bB), tB.k())
            tB_of[j] = tB
            if jl == 3:
                release_item(iC[0])
                release_item(iV[0])
                release_item(iB[0])
            if j >= 1:
                conv3(j - 1)
            conv_units(units, len(units))
            if j == 1 and prev_final[0] is not None:
                prev_final[0]()
                prev_final[0] = None
                need_x[0] = not last
            if j >= 1:
                for _ in range(2):
                    if stat_q:
                        stats(stat_q.pop(0))
        conv3(KC - 1)
        while len(stat_q) > 2:
            stats(stat_q.pop(0))
        nrot[0] = 6

        def ln_finalize():
            ts("dve", lnm.t[:], psb(S1B), 1.0 / D, ALU.mult, psk(S1B), lnm.k())
            tt("dve", lnt.t[:], lnm.t[:], lnm.t[:], ALU.mult, lnm.k(), lnt.k())
            stt(lnv.t[:], psb(S2B), 1.0 / D, lnt.t[:], ALU.mult, ALU.subtract, psk(S2B) + lnt.k(), lnv.k())
            act_op(lnv.t[:], lnv.t[:], AF.Sqrt, lnv.k() + epsb.k(), lnv.k(), bias=epsb.t[:, 1:2])
            SC.op("dve", lambda v: v.reciprocal(out=lnv.t[:], in_=lnv.t[:]), lnv.k(), lnv.k())
            tt("dve", lnr.t[:], lnm.t[:], lnv.t[:], ALU.mult, lnm.k() + lnv.k(), lnr.k())

        def ln_z(j):
            ck = cvo.k(j * T, (j + 1) * T)
            tt("dve", cvo.t[:, j, :], cvo.t[:, j, :], lnv.t[:], ALU.mult, ck + lnv.k(), ck)
            tt("dve", cvo.t[:, j, :], cvo.t[:, j, :], lnr.t[:], ALU.subtract, ck + lnr.k(), ck)

        def ln_silu(j):
            act_op(ubn.t[:, j, :], cvo.t[:, j, :], AF.Silu, cvo.k(j * T, (j + 1) * T) + cols.k(),
                   ubn.k(j * T, (j + 1) * T),
                   scale=cols.t[:, C_LNG + j:C_LNG + j + 1], bias=cols.t[:, C_LNB + j:C_LNB + j + 1])

        if need_x[0]:
            load_x(b + 1)
            need_x[0] = False
        for mm in range(2):
            iG = take_item(("ga", mm))
            iW = take_item(("wa", mm))
            for ml in range(4):
                m = mm * 4 + ml
                bG = proj_group(iG[1], iG[2], ml, hb)
                tG = next_tmp()
                act_op(tG.t[:], psb(bG), AF.Sigmoid, psk(bG), tG.k())
                bY = proj_group(iW[1], iW[2], ml, ya)
                tt("dve", t1.t[:, m, :], psb(bY), tG.t[:], ALU.mult, psk(bY) + tG.k(), t1.k(m * T, (m + 1) * T))
                if m == 1:
                    while stat_q:
                        stats(stat_q.pop(0))
                    ln_finalize()
                if 2 <= m <= 5:
                    ln_z(2 * (m - 2))
                    ln_z(2 * (m - 2) + 1)
                if m == 5:
                    for j in range(KC):
                        ln_silu(j)
            release_item(iG[0])
            release_item(iW[0])

        nxt = xres[(b + 1) % 2]
        for mm in range(2):
            iG = take_item(("gb", mm))
            iW = take_item(("wb", mm))
            if mm == 1 and not last:
                norm_stats(nxt, stats3[0], list(range(NT)))
            for ml in range(4):
                m = mm * 4 + ml
                bG = proj_group(iG[1], iG[2], ml, hb)
                tG = next_tmp()
                act_op(tG.t[:], psb(bG), AF.Sigmoid, psk(bG), tG.k())
                bY = proj_group(iW[1], iW[2], ml, ubn)
                z = zt[m % 2]
                tt("dve", z.t[:], psb(bY), tG.t[:], ALU.mult, psk(bY) + tG.k(), z.k())
                tt("dve", merged.t[:, m, :], z.t[:], t1.t[:, m, :], ALU.add,
                   z.k() + t1.k(m * T, (m + 1) * T), merged.k(m * T, (m + 1) * T))
            release_item(iG[0])
            release_item(iW[0])

        pre = []
        if not last:
            for i in range(31):
                for j in range(4):
                    pre.append((j, i))

        def pre_taps(n):
            for _ in range(n):
                if pre:
                    j, i = pre.pop(0)
                    conv_tap(j, i, 4 + j)

        nrot[0] = 4 if not last else 6
        iG1 = None
        if b == 0:
            iG1 = take_item(("g", 2))
            cur_gate[0] = iG1
        iW0 = take_item(("wo", 0))
        iW1 = take_item(("wo", 1))
        st2 = stats3[1]

        def gated_add(bank, iG, i, half):
            lo, hi = i * D + half * 512, i * D + (half + 1) * 512
            tt("dve", xr.t[:, i, half * 512:(half + 1) * 512], psb(bank), xr.t[:, i, half * 512:(half + 1) * 512],
               ALU.add, psk(bank) + xr.k(lo, hi), xr.k(lo, hi))

        def wo_tile(i):
            for half, iW in enumerate((iW0, iW1)):
                bank = next_bank()
                pairs = [(merged.t[:, kc, i * 128:(i + 1) * 128], iW[2][:, kc, :]) for kc in range(KC)]
                mm_group(psb(bank), pairs, merged.k() + iW[1].k(), psk(bank))
                gated_add(bank, iG1, i, half)
            norm_stats(xr, st2, [i])

        if last:
            for i in range(NT):
                wo_tile(i)
                if i >= 1:
                    norm_transpose(i - 1, h2b, 8, M_SH2)
            norm_transpose(NT - 1, h2b, 8, M_SH2)
        else:
            for i in range(NT):
                norm_transpose(i, hb, 0, M_SH1)
                wo_tile(i)
            release_item(iW0[0])
            release_item(iW1[0])
            glu_it = {}
            for i in range(NT):
                jj = i // 2
                if i % 2 == 0:
                    glu_it[jj] = (take_item(("vc", jj)), take_item(("gl", jj)))
                iVC, iGL = glu_it[jj]
                glu_chunk(2 * i, iVC, iGL)
                if i == NT - 1:
                    norm_transpose(i, h2b, 8, M_SH2)
                    glu_chunk(2 * i + 1, iVC, iGL)
                else:
                    glu_chunk(2 * i + 1, iVC, iGL)
                    norm_transpose(i, h2b, 8, M_SH2)
                if i >= 1:
                    pre_taps(8)
                if i % 2 == 1:
                    release_item(iVC[0])
                    release_item(iGL[0])
        if iG1 is not None:
            release_item(iG1[0])
        if last:
            release_item(iW0[0])
            release_item(iW1[0])

        for jj in range(6):
            iA = take_item(("fa", jj))
            iB = take_item(("fb", jj))
            for jl in range(4 if jj < 5 else 2):
                j = jj * 4 + jl
                bA = proj_group(iA[1], iA[2], jl, h2b)
                tA = next_tmp()
                act_op(tA.t[:], psb(bA), AF.Silu, psk(bA), tA.k())
                bB = proj_group(iB[1], iB[2], jl, h2b)
                tt("dve", actb.t[:, j, :], psb(bB), tA.t[:], ALU.mult, psk(bB) + tA.k(), actb.k(j * T, (j + 1) * T))
                pre_taps(4)
            release_item(iA[0])
            release_item(iB[0])

        for half in range(2):
            if half == 0:
                iG2 = None
                if b == 0:
                    iG2 = take_item(("g", 5))
                    cur_gate[0] = iG2
            its = [take_item(("fo", half, g)) for g in range(3)]
            for i in range(NT):
                bank = next_bank()
                pairs = []
                rk = []
                for g, (n_, r_, v_) in enumerate(its):
                    rk += r_.k()
                    for kk in range(8 if g < 2 else 6):
                        kc = g * 8 + kk
                        pairs.append((actb.t[:, kc, i * 128:(i + 1) * 128], v_[:, kk, :]))
                mm_group(psb(bank), pairs, actb.k() + rk, psk(bank))
                gated_add(bank, iG2, i, half)
                pre_taps(6)
            for it_ in its:
                release_item(it_[0])
        if iG2 is not None:
            release_item(iG2[0])
        pre_taps(len(pre))

        def final_norm():
            st3 = stats3[2]
            for i in range(NT):
                act_op(junk.t[:], xr.t[:, i, :], AF.Square, xr.k(i * D, (i + 1) * D), st3.k(),
                       accum=st3.t[:, i:i + 1])
            ts("pool", st3.t[:, 4:8], st3.t[:, 0:4], 1.0 / D, ALU.mult, st3.k(), st3.k(), s2=EPS, op1=ALU.add)
            tt("pool", st3.t[:, 8:12], st3.t[:, 4:8], nhalf.t[:, 0:4], ALU.pow, st3.k() + nhalf.k(), st3.k())
            for i in range(NT):
                ts("pool", xr.t[:, i, :], xr.t[:, i, :], st3.t[:, 8 + i:9 + i], ALU.mult,
                   xr.k(i * D, (i + 1) * D) + st3.k(), xr.k(i * D, (i + 1) * D), s2=1.0, op1=ALU.mult)
                tt("pool", xr.t[:, i, :], xr.t[:, i, :], GF.t[:], ALU.mult,
                   xr.k(i * D, (i + 1) * D) + GF.k(), xr.k(i * D, (i + 1) * D))
            dma("sp", out[b * T:(b + 1) * T, :].rearrange("(i p) d -> p i d", p=128), xr.t[:], xr.k(),
                [("dr", "out", b)])

        if last:
            final_norm()
        else:
            prev_final[0] = final_norm
        return None

        st3 = stats3[2]
        for i in range(NT):
            act_op(junk.t[:], xr.t[:, i, :], AF.Square, xr.k(i * D, (i + 1) * D), st3.k(),
                   accum=st3.t[:, i:i + 1])
        act_op(st3.t[:, 4:8], st3.t[:, 0:4], AF.Sqrt, st3.k() + epsb.k(), st3.k(), scale=1.0 / D, bias=epsb.t[:, 0:1])
        SC.op("dve", lambda v: v.reciprocal(out=st3.t[:, 8:12], in_=st3.t[:, 4:8]), st3.k(), st3.k())
        for i in range(NT):
            ts("pool", xr.t[:, i, :], xr.t[:, i, :], st3.t[:, 8 + i:9 + i], ALU.mult,
               xr.k(i * D, (i + 1) * D) + st3.k(), xr.k(i * D, (i + 1) * D), s2=1.0, op1=ALU.mult)
            tt("pool", xr.t[:, i, :], xr.t[:, i, :], GF.t[:], ALU.mult,
               xr.k(i * D, (i + 1) * D) + GF.k(), xr.k(i * D, (i + 1) * D))
        return dma("sp", out[b * T:(b + 1) * T, :].rearrange("(i p) d -> p i d", p=128), xr.t[:], xr.k(),
                   [("dr", "out", b)])

    SC.op("dve", lambda v: v.memset(cvb.t[:], 0.0), [], cvb.k())
    SC.op("dve", lambda v: v.memset(ub_.t[:], 0.0), [], ub_.k())
    last_out = None
    for b in range(n_blocks):
        last_out = do_block(b)

    final_reads = [("dr", "out", b) for b in range(n_blocks)]
    SC.op("sp", lambda e: e.nop(), final_reads, [])

    with nc.Block() as block:
        SC.emit(nc, block)
    return nc


_NC_CACHE = {}


def _get_nc(n_blocks):
    if n_blocks not in _NC_CACHE:
        _NC_CACHE[n_blocks] = build_nc(n_blocks)
    return _NC_CACHE[n_blocks]


def kernel(x, c, w_ada, b_ada, norm_mix_g, w_in, conv_short_w, w_short_out,
           conv_conf_w, conv_conf_b, conf_ln_g, conf_ln_b, w_conf_out, w_o,
           norm_ffn_g, w_ffn_in, w_ffn_out, final_norm_g, _n_blocks=None, _cores=None):
    f = lambda a: np.ascontiguousarray(np.asarray(a, dtype=np.float32))
    x = f(x)
    B, S, _ = x.shape
    n_blocks = S // T if _n_blocks is None else _n_blocks
    S_use = n_blocks * T
    cores = list(range(B)) if _cores is None else _cores
    shared = {
        "w_ada": f(w_ada)[0], "b_ada": f(b_ada)[0], "norm_mix_g": f(norm_mix_g)[0], "w_in": f(w_in)[0],
        "conv_short_w": f(conv_short_w)[0], "w_short_out": f(w_short_out)[0], "conv_conf_w": f(conv_conf_w)[0],
        "conv_conf_b": f(conv_conf_b)[0], "conf_ln_g": f(conf_ln_g)[0], "conf_ln_b": f(conf_ln_b)[0],
        "w_conf_out": f(w_conf_out)[0], "w_o": f(w_o)[0], "norm_ffn_g": f(norm_ffn_g)[0],
        "w_ffn_in": f(w_ffn_in)[0], "w_ffn_out": f(w_ffn_out)[0], "final_norm_g": f(final_norm_g),
        "ident_f": np.eye(128, dtype=np.float32),
        "ident_b": np.eye(128, dtype=np.float32).astype(ml_dtypes.bfloat16),
    }
    cc = f(c)
    in_maps = []
    for i in cores:
        m = dict(shared)
        m["x"] = np.ascontiguousarray(x[i, :S_use])
        m["c"] = np.ascontiguousarray(cc[i])
        in_maps.append(m)
    nc = _get_nc(n_blocks)
    res = run_bass_kernel_spmd(nc, in_maps, core_ids=list(range(len(cores))))
    outs = [np.asarray(r["out"], dtype=np.float32) for r in res.results]
    return np.stack(outs, axis=0)
```
